# Optimizing a Trainium2 kernel written in Bass

```python
import jax, jax.numpy as jnp
from jax import lax
import numpy as np

D_MODEL = 1024
BATCH = 8
SEQ = 2048
DEPTH = 1

D_RNN = D_MODEL
N_LRU_HEADS = 16
LRU_HEAD_DIM = D_RNN // N_LRU_HEADS
CONV_WIDTH = 4
LRU_C = 8.0
LRU_A_MIN = 0.9
LRU_A_MAX = 0.999
D_POOL = D_MODEL // 2
POOL_WINDOWS = (2, 4, 8, 16)
N_POOL_GROUPS = len(POOL_WINDOWS)
POOL_GROUP_DIM = D_POOL // N_POOL_GROUPS
N_BRANCHES = 2
D_FF = 4 * D_MODEL
NORM_EPS = 1e-6
D_IN = 2 * D_RNN + D_POOL + N_BRANCHES * D_MODEL

kernel_name = "hawk_pool_gated_hybrid_block"


def rms_norm(x, g):
    xf = x.astype(jnp.float32)
    y = xf * lax.rsqrt(jnp.mean(xf * xf, axis=-1, keepdims=True) + NORM_EPS)
    return (y * g.astype(jnp.float32)).astype(x.dtype)


def causal_depthwise_conv(x, w, b):
    S = x.shape[1]
    xp = jnp.pad(x, ((0, 0), (CONV_WIDTH - 1, 0), (0, 0)))
    out = b
    for k in range(CONV_WIDTH):
        out = out + xp[:, k:k + S] * w[k]
    return out


def rg_lru(x, w_a, b_a, w_x, b_x, lam):
    B, S, _ = x.shape
    xh = x.reshape(B, S, N_LRU_HEADS, LRU_HEAD_DIM)
    r = jnp.einsum('bshi,hij->bshj', xh, w_a).reshape(B, S, D_RNN) + b_a
    i = jnp.einsum('bshi,hij->bshj', xh, w_x).reshape(B, S, D_RNN) + b_x
    r = jax.nn.sigmoid(r.astype(jnp.float32))
    i = jax.nn.sigmoid(i.astype(jnp.float32))
    log_a = -LRU_C * r * jax.nn.softplus(-lam.astype(jnp.float32))
    a = jnp.exp(log_a)
    mult = jnp.sqrt(-jnp.expm1(2.0 * log_a))
    u = mult * (i * x.astype(jnp.float32))

    def combine(left, right):
        a1, b1 = left
        a2, b2 = right
        return a1 * a2, a2 * b1 + b2

    _, h = lax.associative_scan(combine, (a, u), axis=1)
    return h.astype(x.dtype)


def multiscale_pool(x, w_grp, scale):
    B, S, _ = x.shape
    xf = x.astype(jnp.float32)
    cs = jnp.cumsum(xf, axis=1)
    pos = jnp.arange(1, S + 1, dtype=jnp.float32)[None, :, None]
    outs = []
    for g, w in enumerate(POOL_WINDOWS):
        sl = slice(g * POOL_GROUP_DIM, (g + 1) * POOL_GROUP_DIM)
        c = cs[..., sl]
        c_prev = jnp.pad(c, ((0, 0), (w, 0), (0, 0)))[:, :S]
        cnt = jnp.minimum(pos, float(w))
        outs.append((c - c_prev) / cnt - xf[..., sl])
    p = jnp.stack(outs, axis=2)
    y = jnp.einsum('bsgi,gij->bsgj', p, w_grp.astype(jnp.float32)).reshape(B, S, D_POOL)
    return (y * scale.astype(jnp.float32)).astype(x.dtype)


def hybrid_mixer(h, w_in, b_gate, conv_w, conv_b, lru_w_a, lru_b_a, lru_w_x, lru_b_x,
                 lru_lambda, pool_w, pool_scale, w_lru_up, w_pool_up, w_o):
    proj = h @ w_in
    x_lru, g_lru, x_pool, gates = jnp.split(
        proj, [D_RNN, 2 * D_RNN, 2 * D_RNN + D_POOL], axis=-1)
    x_lru = causal_depthwise_conv(x_lru, conv_w, conv_b)
    y_lru = rg_lru(x_lru, lru_w_a, lru_b_a, lru_w_x, lru_b_x, lru_lambda) * jax.nn.gelu(g_lru)
    y_pool = multiscale_pool(x_pool, pool_w, pool_scale)
    br_a = y_lru @ w_lru_up
    br_b = y_pool @ w_pool_up
    gate_a, gate_b = jnp.split(jax.nn.sigmoid(gates + b_gate), N_BRANCHES, axis=-1)
    return (gate_a * br_a + gate_b * br_b) @ w_o


def sq_relu_mlp(h, w_ff1, w_ff2):
    return jnp.square(jax.nn.relu(h @ w_ff1)) @ w_ff2


def setup_inputs(seed: int = 0) -> dict:
    key = jax.random.key(seed)
    ks = jax.random.split(key, 24)
    L = DEPTH
    f32 = jnp.float32

    def nrm(k, shape, fan_in):
        return jax.random.normal(k, shape, f32) * (fan_in ** -0.5)

    def gain(k, shape):
        return 1.0 + 0.02 * jax.random.normal(k, shape, f32)

    def bias(k, shape):
        return 0.01 * jax.random.normal(k, shape, f32)

    u = jax.random.uniform(ks[12], (L, D_RNN), f32, LRU_A_MIN, LRU_A_MAX)
    a_base = u ** (1.0 / LRU_C)
    lru_lambda = jnp.log(a_base) - jnp.log1p(-a_base)

    return {
        "x": jax.random.normal(ks[0], (BATCH, SEQ, D_MODEL), f32),
        "norm_mix_pre": gain(ks[1], (L, D_MODEL)),
        "norm_mix_post": gain(ks[2], (L, D_MODEL)),
        "norm_mlp_pre": gain(ks[3], (L, D_MODEL)),
        "norm_mlp_post": gain(ks[4], (L, D_MODEL)),
        "w_in": nrm(ks[5], (L, D_MODEL, D_IN), D_MODEL),
        "b_gate": bias(ks[6], (L, N_BRANCHES * D_MODEL)),
        "conv_w": nrm(ks[7], (L, CONV_WIDTH, D_RNN), CONV_WIDTH),
        "conv_b": bias(ks[8], (L, D_RNN)),
        "lru_w_a": nrm(ks[9], (L, N_LRU_HEADS, LRU_HEAD_DIM, LRU_HEAD_DIM), LRU_HEAD_DIM),
        "lru_b_a": bias(ks[10], (L, D_RNN)),
        "lru_w_x": nrm(ks[11], (L, N_LRU_HEADS, LRU_HEAD_DIM, LRU_HEAD_DIM), LRU_HEAD_DIM),
        "lru_b_x": bias(ks[13], (L, D_RNN)),
        "lru_lambda": lru_lambda,
        "pool_w": nrm(ks[14], (L, N_POOL_GROUPS, POOL_GROUP_DIM, POOL_GROUP_DIM), POOL_GROUP_DIM),
        "pool_scale": gain(ks[15], (L, D_POOL)),
        "w_lru_up": nrm(ks[16], (L, D_RNN, D_MODEL), D_RNN),
        "w_pool_up": nrm(ks[17], (L, D_POOL, D_MODEL), D_POOL),
        "w_o": nrm(ks[18], (L, D_MODEL, D_MODEL), D_MODEL),
        "w_ff1": nrm(ks[19], (L, D_MODEL, D_FF), D_MODEL),
        "w_ff2": nrm(ks[20], (L, D_FF, D_MODEL), D_FF),
    }


def reference(x, norm_mix_pre, norm_mix_post, norm_mlp_pre, norm_mlp_post, w_in, b_gate,
              conv_w, conv_b, lru_w_a, lru_b_a, lru_w_x, lru_b_x, lru_lambda, pool_w,
              pool_scale, w_lru_up, w_pool_up, w_o, w_ff1, w_ff2):
    for l in range(DEPTH):
        h = rms_norm(x, norm_mix_pre[l])
        m = hybrid_mixer(h, w_in[l], b_gate[l], conv_w[l], conv_b[l], lru_w_a[l], lru_b_a[l],
                         lru_w_x[l], lru_b_x[l], lru_lambda[l], pool_w[l], pool_scale[l],
                         w_lru_up[l], w_pool_up[l], w_o[l])
        x = x + rms_norm(m, norm_mix_post[l])
        h = rms_norm(x, norm_mlp_pre[l])
        f = sq_relu_mlp(h, w_ff1[l], w_ff2[l])
        x = x + rms_norm(f, norm_mlp_post[l])
    return x
```

```python
import numpy as np
from contextlib import ExitStack
import concourse.bass as bass
import concourse.mybir as mybir
from concourse.bass_utils import run_bass_kernel_spmd

F32 = mybir.dt.float32
BF16 = mybir.dt.bfloat16
AF = mybir.ActivationFunctionType
ALU = mybir.AluOpType

T = 2048
D = 1024
NCORES = 8
EPS = 1e-6
K = 1024


class Prog:
    ENGS = ("sp", "act", "pool", "dve", "pe")
    MAX_SWDGE = 4

    def __init__(self, nc, es):
        self.nc = nc
        self.es = es
        self.ops = []
        self.lw = {}
        self.rd = {}
        self.dma_names = []
        self.floors = {}
        self.pool_dmas = []

    def add(self, eng, fn, reads=(), writes=(), dma=None):
        deps = set()
        for k in list(reads) + list(writes):
            f = self.floors.get(k[0])
            if f is not None:
                deps.add(f)
        for k in reads:
            if k in self.lw:
                deps.add(self.lw[k])
        for k in writes:
            if k in self.lw:
                deps.add(self.lw[k])
            deps |= self.rd.get(k, set())
        idx = len(self.ops)
        if dma is not None and eng == "pool":
            if len(self.pool_dmas) >= self.MAX_SWDGE:
                deps.add(self.pool_dmas[-self.MAX_SWDGE])
            self.pool_dmas.append(idx)
        self.ops.append(dict(eng=eng, fn=fn, deps=deps, dma=dma, sig=False))
        if dma is not None and dma not in self.dma_names:
            self.dma_names.append(dma)
        for k in reads:
            self.rd.setdefault(k, set()).add(idx)
        for k in writes:
            self.lw[k] = idx
            self.rd[k] = set()
        return idx

    def fence(self, fn, old_names, new_names, eng="dve"):
        deps = set()
        old = set(old_names)
        for k, i in self.lw.items():
            if k[0] in old:
                deps.add(i)
        for k, s in self.rd.items():
            if k[0] in old:
                deps |= s
        for n in old:
            f = self.floors.get(n)
            if f is not None:
                deps.add(f)
        idx = len(self.ops)
        self.ops.append(dict(eng=eng, fn=fn, deps=deps, dma=None, sig=False))
        for n in new_names:
            self.floors[n] = idx
        return idx

    def emit(self, final_wait=()):
        nc, es = self.nc, self.es
        ops = self.ops

        def skip(po, c):
            return po["dma"] is None and c["dma"] is None and po["eng"] == "pe" and c["eng"] == "pe"

        for c in ops:
            for p in c["deps"]:
                if not skip(ops[p], c):
                    ops[p]["sig"] = True
        esem = {e: es.enter_context(nc.semaphore("s_" + e)) for e in self.ENGS}
        dsem = {n: es.enter_context(nc.semaphore("d_" + str(n))) for n in self.dma_names}
        cnt = {e: 0 for e in self.ENGS}
        dcnt = {n: 0 for n in self.dma_names}
        for o in ops:
            if o["dma"] is not None:
                dcnt[o["dma"]] += 16
                o["sem"], o["val"] = dsem[o["dma"]], dcnt[o["dma"]]
            elif o["sig"]:
                cnt[o["eng"]] += 1
                o["sem"], o["val"] = esem[o["eng"]], cnt[o["eng"]]
        per = {e: [o for o in ops if o["eng"] == e] for e in self.ENGS}

        def run(e, eng):
            waited = {}
            for o in per[e]:
                need = {}
                for p in o["deps"]:
                    po = ops[p]
                    if "sem" not in po or skip(po, o):
                        continue
                    s = po["sem"]
                    if po["val"] > need.get(id(s), (None, 0))[1]:
                        need[id(s)] = (s, po["val"])
                for key, (s, v) in need.items():
                    if waited.get(key, 0) >= v:
                        continue
                    eng.wait_ge(s, v)
                    waited[key] = v
                ins = o["fn"](eng)
                if "sem" in o:
                    ins.then_inc(o["sem"], 16 if o["dma"] is not None else 1)
            if e == "sp":
                for n in final_wait:
                    eng.wait_ge(dsem[n], dcnt[n])

        with nc.Block() as block:
            @block.sync
            def _(eng):
                run("sp", eng)

            @block.scalar
            def _(eng):
                run("act", eng)

            @block.gpsimd
            def _(eng):
                run("pool", eng)

            @block.vector
            def _(eng):
                run("dve", eng)

            @block.tensor
            def _(eng):
                run("pe", eng)


def build_nc(stop=None):
    nc = bass.Bass("TRN2", target_bir_lowering=False)
    es = ExitStack()

    def dram(name, shape, kind="ExternalInput"):
        return nc.dram_tensor(name, shape, F32, kind=kind).ap()

    X = dram("x", [T, D])
    OUT = dram("out", [T, D], "ExternalOutput")
    WLRU = dram("wlru", [8, 128, 2048])
    WPOOL = dram("wpool", [4, 128, 1024])
    WMRGA = dram("wmrga", [8, 128, 2048])
    WMRGB = dram("wmrgb", [8, 128, 1536])
    WO = dram("wo", [128, 8192])
    WFF1 = dram("wff1", [32, 128, 1024])
    WFF2 = dram("wff2", [128, 32768])
    WABD = dram("wabd", [128, 1024])
    WXBD = dram("wxbd", [128, 1024])
    CONVD = dram("convd", [128, 4096])
    POOLW = dram("poolw", [128, 512])
    IDENT = dram("ident", [128, 128])
    CVEC = dram("colvecs", [128, 52])
    GAINS = dram("gains", [4, D])
    X1S = dram("x1s", [T, D], "Internal")

    def sb(name, shape, dt):
        return es.enter_context(nc.sbuf_tensor(name, shape, dt))

    cvt = sb("cvt", [128, 52], F32)
    dv = sb("dv", [128, 64], F32)
    st = sb("st", [128, 4 * 48], F32)
    icnt = sb("icnt", [128, 16], F32)
    gA = sb("gA", [128, D], F32)
    gB = sb("gB", [128, D], F32)
    identb = sb("identb", [128, 128], BF16)
    junk = sb("junk", [128, D], BF16)
    fsc = sb("fsc", [128, 8], F32)
    hT = sb("hT", [128, 8, T], BF16)
    BIGN = 160 * 256
    big = sb("big", [128, BIGN], F32)
    pps = [es.enter_context(nc.psum_tensor("pp%d" % i, [128, 1024], F32)) for i in range(4)]

    def bank(i):
        return pps[i // 2][:, (i % 2) * 512:(i % 2) * 512 + 512]

    def bank_bf(i):
        return bank(i).bitcast(BF16)

    def carve(off, nbytes, dt=F32, pat=None, **kw):
        assert off % 4 == 0 and nbytes % 4 == 0 and off + nbytes <= BIGN * 4, (off, nbytes)
        ap = big[:, off // 4:(off + nbytes) // 4]
        if dt == BF16:
            ap = ap.bitcast(BF16)
        if pat is not None:
            ap = ap.rearrange(pat, **kw)
        return ap

    C_CONVB, C_BA, C_BX, C_LAM, C_PSC, C_BG = 0, 8, 16, 24, 32, 36
    D_HBA, D_HBX, D_HBG, D_CH, D_T0, D_EPS1, D_EPS4 = 0, 8, 16, 32, 40, 48, 49

    P = Prog(nc, es)

    def finish():
        P.emit(final_wait=[n for n in ("out0_0", "out0_1", "out1_0", "out1_1") if n in P.dma_names])
        return nc

    def PB(i):
        return ("pb", i)

    yT = carve(0, 32 * K, BF16, "p (a b) -> p a b", b=T)
    mergedT = carve(64 * K, 32 * K, BF16, "p (a b) -> p a b", b=T)
    ypT = carve(96 * K, 16 * K, BF16, "p (a b) -> p a b", b=T)
    ring = [carve(112 * K + s * 8 * K, 8 * K, BF16) for s in range(2)]
    woS = carve(128 * K, 16 * K, BF16, "p (a b) -> p a b", b=D)
    wabd = carve(144 * K, 2 * K, BF16, "p (a b) -> p a b", b=128)
    wxbd = carve(146 * K, 2 * K, BF16, "p (a b) -> p a b", b=128)
    convd = carve(148 * K, 8 * K, BF16, "p (a b) -> p a b", b=128)
    poolw = carve(156 * K, 1 * K, BF16, "p (a b) -> p a b", b=128)
    xinA = [carve(128 * K + s * 4 * K, 4 * K) for s in range(3)]
    hbA = [carve(140 * K + s * 2 * K, 2 * K, BF16) for s in range(2)]

    xin_first = [carve(128 * K + s * 4 * K, 4 * K) for s in range(2)]
    for tt0 in range(2):
        P.add("sp", lambda e, tt0=tt0: e.dma_start(out=xin_first[tt0], in_=X[tt0 * 128:(tt0 + 1) * 128, :]),
              writes=[("xinA", tt0)], dma="xinA%d" % tt0)
    P.add("sp", lambda e: e.dma_start(out=cvt[:], in_=CVEC), writes=[("cvt",)], dma="cv")
    P.add("sp", lambda e: e.dma_start(out=gA[:], in_=GAINS[0:1, :].partition_broadcast(128)),
          writes=[("gA",)], dma="gA")
    P.add("pool", lambda e: e.dma_start(out=identb[:], in_=IDENT), writes=[("ident",)], dma="ident")

    def late_const_loads(after_key):
        P.add("pool", lambda e: e.dma_start(out=convd, in_=CONVD.rearrange("p (a b) -> p a b", b=128)),
              reads=[after_key], writes=[("convd",)], dma="convd")
        P.add("pool", lambda e: e.dma_start(out=wabd, in_=WABD.rearrange("p (a b) -> p a b", b=128)),
              writes=[("wabd",)], dma="wabd")
        P.add("pool", lambda e: e.dma_start(out=wxbd, in_=WXBD.rearrange("p (a b) -> p a b", b=128)),
              writes=[("wxbd",)], dma="wxbd")
        P.add("pool", lambda e: e.dma_start(out=poolw, in_=POOLW.rearrange("p (a b) -> p a b", b=128)),
              writes=[("poolw",)], dma="poolw")

    def setup_dv(e):
        e.memset(dv[:, D_EPS1:D_EPS1 + 1], EPS)
        e.memset(dv[:, D_EPS4:D_EPS4 + 1], 4.0 * EPS)
        for t in range(16):
            e.memset(icnt[:, t:t + 1], 1.0 / (t + 1))
        e.tensor_scalar(out=dv[:, D_HBA:D_HBA + 16], in0=cvt[:, C_BA:C_BA + 16], scalar1=0.5, scalar2=None,
                        op0=ALU.mult)
        return e.tensor_scalar(out=dv[:, D_HBG:D_HBG + 16], in0=cvt[:, C_BG:C_BG + 16], scalar1=0.5,
                               scalar2=None, op0=ALU.mult)
    P.add("dve", setup_dv, reads=[("cvt",)], writes=[("dv",)])
    def softplus_setup():
        P.add("act", lambda e: e.activation(out=dv[:, D_T0:D_T0 + 8], in_=cvt[:, C_LAM:C_LAM + 8], func=AF.Exp,
                                            scale=-1.0), reads=[("cvt",)], writes=[("dvt",)])
        P.add("act", lambda e: e.activation(out=dv[:, D_T0:D_T0 + 8], in_=dv[:, D_T0:D_T0 + 8], func=AF.Ln,
                                            bias=1.0), reads=[("dvt",)], writes=[("dvt",)])
        P.add("dve", lambda e: e.tensor_scalar(out=dv[:, D_CH:D_CH + 8], in0=dv[:, D_T0:D_T0 + 8], scalar1=-4.0,
                                               scalar2=None, op0=ALU.mult), reads=[("dvt",)], writes=[("dvc",)])

    def rstd_act(n, col, eps_col, ss_key):
        base = n * 48
        P.add("act", lambda e: e.activation(out=st[:, base + 16 + col:base + 17 + col],
                                            in_=st[:, base + col:base + col + 1], func=AF.Sqrt,
                                            scale=1.0 / D, bias=dv[:, eps_col:eps_col + 1]),
              reads=[ss_key, ("dv",)], writes=[("sd", n, col)])

    def rstd_dve(n, col):
        base = n * 48
        P.add("dve", lambda e: e.reciprocal(out=st[:, base + 32 + col:base + 33 + col],
                                            in_=st[:, base + 16 + col:base + 17 + col]),
              reads=[("sd", n, col)], writes=[("rs", n, col)])
        return st[:, base + 32 + col:base + 33 + col]

    def rstd_ops(n, col, eps_col, ss_key):
        rstd_act(n, col, eps_col, ss_key)
        return rstd_dve(n, col)

    def transposes_to_hT(src_ap, src_key, tt, bk, act_only=False, part="both"):
        if part in ("both", "pe"):
            transposes_pe(src_ap, src_key, bk)
        if part in ("both", "evac"):
            transposes_evac(tt, bk, act_only)

    def transposes_pe(src_ap, src_key, bk):
        def tr(e):
            ins = None
            for c in range(8):
                ins = e.transpose(out=bank_bf(bk)[:, c * 128:(c + 1) * 128], in_=src_ap[:, c * 128:(c + 1) * 128],
                                  identity=identb[:])
            return ins
        P.add("pe", tr, reads=[src_key, ("ident",)], writes=[PB(bk)])

    def transposes_evac(tt, bk, act_only):
        outap = hT[:, :, tt * 128:(tt + 1) * 128]
        inap = bank_bf(bk).rearrange("p (a b) -> p a b", b=128)
        wkeys = [("hT", c, tt // 4) for c in range(8)]
        if tt % 2 == 0 or act_only:
            P.add("act", lambda e: e.activation(out=outap, in_=inap, func=AF.Copy), reads=[PB(bk)], writes=wkeys)
        else:
            P.add("dve", lambda e: e.tensor_copy(out=outap, in_=inap), reads=[PB(bk)], writes=wkeys)

    NTT = T // 128
    WKB = 32 * K
    xinA6 = xinA + [carve(s * 4 * K, 4 * K) for s in range(3)]

    def xkeyA(s):
        return ("xinA", s) if s < 3 else ("xinA2", s)

    def A_stage1(tt):
        s = tt % 6
        xin = xinA6[s]
        if tt >= 2:
            P.add("sp", lambda e: e.dma_start(out=xin, in_=X[tt * 128:(tt + 1) * 128, :]),
                  writes=[xkeyA(s)], dma="xinA%d" % s)
        if tt == 4:
            late_const_loads(xkeyA(3))
        if tt == 6:
            softplus_setup()
        P.add("act", lambda e: e.activation(out=junk[:], in_=xin, func=AF.Square, accum_out=st[:, tt:tt + 1]),
              reads=[xkeyA(s)], writes=[("ss", 0, tt)])
        rstd_act(0, tt, D_EPS1, ("ss", 0, tt))

    def A_stage2(tt):
        s = tt % 6
        xin = xinA6[s]
        rs = rstd_dve(0, tt)
        hb = hbA[tt % 2]
        P.add("dve", lambda e: e.scalar_tensor_tensor(out=hb, in0=xin, scalar=rs, in1=gA[:], op0=ALU.mult,
                                                      op1=ALU.mult),
              reads=[xkeyA(s), ("rs", 0, tt), ("gA",)], writes=[("hbA", tt % 2)])
        transposes_to_hT(hb, ("hbA", tt % 2), tt, tt % 2, part="pe")

    def A_stage3(tt):
        transposes_to_hT(None, None, tt, tt % 2, part="evac")

    for sidx in range(NTT + 2):
        if sidx < NTT:
            A_stage1(sidx)
        if 1 <= sidx <= NTT:
            A_stage2(sidx - 1)
        if sidx >= 2:
            A_stage3(sidx - 2)

    if stop == "A":
        return finish()
    P.add("sp", lambda e: e.dma_start(out=gB[:], in_=GAINS[1:2, :].partition_broadcast(128)),
          writes=[("gB",)], dma="gB")
    P.add("sp", lambda e: e.dma_start(out=gA[:], in_=GAINS[2:3, :].partition_broadcast(128)),
          writes=[("gA",)], dma="gA")
    P.fence(lambda e: e.memset(fsc[:, 6:7], 0.0), ["xinA2"], ["yT"])

    bufA = [carve(WKB + p * 8 * K, 8 * K) for p in range(2)]
    bufI = [carve(WKB + 16 * K + p * 8 * K, 8 * K) for p in range(2)]
    gel = [carve(WKB + 32 * K + p * 8 * K, 8 * K) for p in range(3)]
    bufM = carve(WKB + 56 * K, 8 * K)
    xcr = [carve(WKB + 64 * K + p * 2 * K, 2 * K) for p in range(2)]
    hhT = bufM
    xlb = carve(WKB + 68 * K, 4608, BF16)
    xcb = [carve(WKB + 73 * K + p * K, K, BF16) for p in range(2)]

    P.add("dve", lambda e: e.memset(xlb[:, 0:4], 0.0), writes=[("xlbpad",)])

    def CH(q):
        return slice(q * 512, (q + 1) * 512)

    def load_ring(slot, src_ap, n_el, key):
        dst = ring[slot][:, 0:n_el]
        P.add("pool", lambda e: e.dma_start(out=dst, in_=src_ap), writes=[("ring", slot)], dma="ring%d" % slot)

    rc = 0
    lru_blk = {}

    def B_stage1(u):
        nonlocal rc
        c, q = u // 4, u % 4
        par = c % 2
        if q == 0:
            for cc in (c, c + 1):
                if cc < 8 and cc not in lru_blk:
                    slot = rc % 2
                    rc += 1
                    load_ring(slot, WLRU[cc], 2048, None)
                    lru_blk[cc] = (slot, ring[slot][:, 0:2048].rearrange("p (k n) -> p k n", n=256))
        slot, wblk = lru_blk[c]
        bXA, bGB = u % 2, 2 + u % 2

        def mm_x(e):
            ins = None
            for k in range(8):
                ins = e.matmul(bank(bXA), lhsT=wblk[:, k, 0:128], rhs=hT[:, k, CH(q)], start=(k == 0), stop=(k == 7))
            for k in range(8):
                ins = e.matmul(bank(bGB), lhsT=wblk[:, k, 128:256], rhs=hT[:, k, CH(q)], start=(k == 0), stop=(k == 7))
            return ins
        P.add("pe", mm_x, reads=[("ring", slot)] + [("hT", k, q) for k in range(8)], writes=[PB(bXA), PB(bGB)])
        P.add("dve", lambda e: e.tensor_copy(out=xlb[:, 3 + q * 512:3 + (q + 1) * 512], in_=bank(bXA)),
              reads=[PB(bXA), ("xlbpad",)], writes=[("xlb", q)])
        P.add("act", lambda e: e.activation(out=gel[c % 3][:, CH(q)], in_=bank(bGB), func=AF.Gelu_apprx_tanh),
              reads=[PB(bGB)], writes=[("gel", c % 3, q)])

    def B_stage2(u):
        c, q = u // 4, u % 4
        bCV = 4 + u % 2

        def mm_conv(e):
            ins = None
            for k4 in range(4):
                ins = e.matmul(bank(bCV), lhsT=convd[:, c * 4 + k4, :], rhs=xlb[:, q * 512 + k4:q * 512 + k4 + 512],
                               start=(k4 == 0), stop=(k4 == 3))
            return ins
        P.add("pe", mm_conv, reads=[("convd",), ("xlb", q)] + ([("xlb", q - 1)] if q > 0 else [("xlbpad",)]),
              writes=[PB(bCV)])
        xc = xcr[u % 2]
        P.add("dve", lambda e: e.tensor_scalar(out=xc, in0=bank(bCV), scalar1=cvt[:, C_CONVB + c:C_CONVB + c + 1],
                                               scalar2=None, op0=ALU.add),
              reads=[PB(bCV), ("cvt",)], writes=[("xc", u % 2)])
        P.add("dve", lambda e: e.tensor_copy(out=xcb[u % 2], in_=xc), reads=[("xc", u % 2)], writes=[("xcb", u % 2)])

    def B_stage3(u):
        c = u // 4

        def mm_g(e):
            e.matmul(bank(6), lhsT=wabd[:, c, :], rhs=xcb[u % 2], start=True, stop=True)
            return e.matmul(bank(7), lhsT=wxbd[:, c, :], rhs=xcb[u % 2], start=True, stop=True)
        P.add("pe", mm_g, reads=[("wabd",), ("wxbd",), ("xcb", u % 2)], writes=[PB(6), PB(7)])

    def B_stage4(u):
        c, q = u // 4, u % 4
        par = c % 2
        xc = xcr[u % 2]
        P.add("act", lambda e: e.activation(out=bufA[par][:, CH(q)], in_=bank(6), func=AF.Tanh, scale=0.5,
                                            bias=dv[:, D_HBA + c:D_HBA + c + 1]),
              reads=[PB(6), ("dv",)], writes=[("bufA", par, q)])
        P.add("act", lambda e: e.activation(out=bufI[par][:, CH(q)], in_=bank(7), func=AF.Tanh, scale=0.5,
                                            bias=dv[:, D_HBX + c:D_HBX + c + 1]),
              reads=[PB(7), ("dv",)], writes=[("bufI", par, q)])
        P.add("dve", lambda e: e.scalar_tensor_tensor(out=bufI[par][:, CH(q)], in0=bufI[par][:, CH(q)], scalar=1.0,
                                                      in1=xc, op0=ALU.add, op1=ALU.mult),
              reads=[("bufI", par, q), ("xc", u % 2)], writes=[("bufI", par, q)])

    def B_back1(c):
        par = c % 2
        for q in range(4):
            P.add("act", lambda e, q=q: e.activation(out=bufA[par][:, CH(q)], in_=bufA[par][:, CH(q)], func=AF.Exp,
                                                     scale=dv[:, D_CH + c:D_CH + c + 1], bias=dv[:, D_CH + c:D_CH + c + 1]),
                  reads=[("bufA", par, q), ("dvc",)], writes=[("bufA", par, q)])
        for q in range(4):
            P.add("act", lambda e, q=q: e.activation(out=bufM[:, CH(q)], in_=bufA[par][:, CH(q)], func=AF.Square),
                  reads=[("bufA", par, q)], writes=[("bufM", q)])

    def B_back2(c):
        par = c % 2
        for q in range(4):
            P.add("act", lambda e, q=q: e.activation(out=bufM[:, CH(q)], in_=bufM[:, CH(q)], func=AF.Sqrt, scale=-1.0,
                                                     bias=1.0),
                  reads=[("bufM", q)], writes=[("bufM", q)])
        for q in range(4):
            P.add("dve", lambda e, q=q: e.tensor_tensor(out=bufI[par][:, CH(q)], in0=bufI[par][:, CH(q)],
                                                        in1=bufM[:, CH(q)], op=ALU.mult),
                  reads=[("bufI", par, q), ("bufM", q)], writes=[("bufI", par, q)])

    def B_back3(c):
        par = c % 2
        P.add("dve", lambda e: e.tensor_tensor_scan(out=hhT, data0=bufA[par], data1=bufI[par], initial=0.0,
                                                    op0=ALU.mult, op1=ALU.add),
              reads=[("bufA", par, q) for q in range(4)] + [("bufI", par, q) for q in range(4)]
              + [("bufM", q) for q in range(4)], writes=[("bufM", q) for q in range(4)])
        P.add("dve", lambda e: e.scalar_tensor_tensor(out=yT[:, c, :], in0=hhT, scalar=0.5, in1=gel[c % 3],
                                                      op0=ALU.mult, op1=ALU.mult),
              reads=[("bufM", q) for q in range(4)] + [("gel", c % 3, q) for q in range(4)],
              writes=[("yT", c, q) for q in range(4)])

    NU = 32
    for sidx in range(NU + 8):
        u4 = sidx - 3
        if 0 <= u4 < NU:
            B_stage4(u4)
            if u4 % 4 == 3:
                B_back1(u4 // 4)
        u5 = sidx - 4
        if 0 <= u5 < NU and u5 % 4 == 3:
            B_back2(u5 // 4)
        u6 = sidx - 6
        if 0 <= u6 < NU and u6 % 4 == 3:
            B_back3(u6 // 4)
        if sidx < NU:
            B_stage1(sidx)
        if 0 <= sidx - 1 < NU:
            B_stage2(sidx - 1)
        if 0 <= sidx - 2 < NU:
            B_stage3(sidx - 2)

    WINS = (2, 4, 8, 16)
    GORD = (3, 2, 1, 0)
    pblk = carve(157 * K, 2 * K, BF16, "p (k n) -> p k n", n=128)
    tga = [carve(144 * K + p * 2 * K, 2 * K) for p in range(2)]
    pbfs = [carve(148 * K + i * 4 * K, 4 * K, BF16) for i in range(2)]
    XPW = 2064
    XPB = 8448
    xp1 = carve(WKB, XPW * 4)
    sA = carve(WKB + XPB, XPW * 4)
    sB = carve(WKB + 2 * XPB, XPW * 4)
    tgb = [carve(WKB + 3 * XPB + p * 2 * K, 2 * K) for p in range(2)]
    m2b = carve(WKB + 3 * XPB + 4 * K, 2 * K)

    def C_mm(i):
        g = GORD[i]
        P.add("pool", lambda e: e.dma_start(out=pblk, in_=WPOOL[g].rearrange("p (k n) -> p k n", n=128)),
              writes=[("pblk",)], dma="pblk")
        for q in range(4):
            bk = 4 + q

            def mm_p(e, q=q, bk=bk):
                ins = None
                for k in range(8):
                    ins = e.matmul(bank(bk), lhsT=pblk[:, k, :], rhs=hT[:, k, CH(q)], start=(k == 0), stop=(k == 7))
                return ins
            P.add("pe", mm_p, reads=[("pblk",)] + [("hT", k, q) for k in range(8)], writes=[PB(bk)])

    def C_evac(i):
        for q in range(4):
            bk = 4 + q
            P.add("act", lambda e, q=q, bk=bk: e.activation(out=xp1[:, 16 + q * 512:16 + (q + 1) * 512], in_=bank(bk),
                                                           func=AF.Copy),
                  reads=[PB(bk), ("xppad",)], writes=[("xp", q)])

    def C_pool_ops(i):
        g = GORD[i]
        w = WINS[g]
        pbf = pbfs[i % 2]
        xk = [("xp", q) for q in range(4)]
        ops = []
        src, srck = xp1, xk
        sh = 1
        bufs = [(sA, [("sA", q) for q in range(4)]), (sB, [("sB", q) for q in range(4)])]
        bi = 0
        while sh < w:
            dst, dstk = bufs[bi % 2]
            ops.append(lambda src=src, dst=dst, sh=sh, srck=srck, dstk=dstk: P.add(
                "dve", lambda e: e.tensor_tensor(out=dst[:, 16:16 + T], in0=src[:, 16:16 + T],
                                                 in1=src[:, 16 - sh:16 - sh + T], op=ALU.add),
                reads=list(srck) + [("xppad",)], writes=list(dstk)))
            src, srck = dst, dstk
            sh *= 2
            bi += 1
        ops.append(lambda src=src, srck=srck: P.add(
            "dve", lambda e: e.scalar_tensor_tensor(out=pbf, in0=src[:, 16:16 + T], scalar=1.0 / w,
                                                    in1=xp1[:, 16:16 + T], op0=ALU.mult, op1=ALU.subtract),
            reads=list(srck) + xk, writes=[("pbf", i % 2, q) for q in range(4)]))
        scr, okey = bufs[bi % 2]

        def fixops(src=src, srck=srck, scr=scr, okey=okey):
            P.add("dve", lambda e: e.tensor_tensor(out=scr[:, 16:16 + w - 1], in0=src[:, 16:16 + w - 1],
                                                   in1=icnt[:, 0:w - 1], op=ALU.mult),
                  reads=[srck[0], ("dv",)], writes=[okey[0]])
            P.add("dve", lambda e: e.tensor_tensor(out=pbf[:, 0:w - 1], in0=scr[:, 16:16 + w - 1],
                                                   in1=xp1[:, 16:16 + w - 1], op=ALU.subtract),
                  reads=[okey[0], xk[0]], writes=[("pbf", i % 2, 0)])
        ops.append(fixops)
        return ops

    def C_y(i):
        g = GORD[i]
        pbf = pbfs[i % 2]
        for q in range(4):
            bk = 4 + q
            P.add("pe", lambda e, q=q, bk=bk: e.matmul(bank(bk), lhsT=poolw[:, g, :], rhs=pbf[:, CH(q)],
                                                      start=True, stop=True),
                  reads=[("poolw",), ("pbf", i % 2, q), ("pbf", i % 2, 0)], writes=[PB(bk)])
            P.add("act", lambda e, q=q, bk=bk: e.activation(out=ypT[:, g, CH(q)], in_=bank(bk), func=AF.Copy,
                                                           scale=cvt[:, C_PSC + g:C_PSC + g + 1]),
                  reads=[PB(bk), ("cvt",)], writes=[("ypT", g, q)])

    mrgA = {}

    def D1_load(j):
        nonlocal rc
        if j < 8 and j not in mrgA:
            slot = rc % 2
            rc += 1
            load_ring(slot, WMRGA[j], 2048, None)
            mrgA[j] = (slot, ring[slot][:, 0:2048].rearrange("p (k n) -> p k n", n=128))

    def D1_ga(u):
        j, q = u // 4, u % 4
        slot, wblk = mrgA[j]
        bGA = u % 2

        def mm(e):
            ins = None
            for k in range(8):
                ins = e.matmul(bank(bGA), lhsT=wblk[:, k, :], rhs=hT[:, k, CH(q)], start=(k == 0), stop=(k == 7))
            return ins
        P.add("pe", mm, reads=[("ring", slot)] + [("hT", k, q) for k in range(8)], writes=[PB(bGA)])

    def D1_rest(u, with_ga=False):
        j, q = u // 4, u % 4
        slot, wblk = mrgA[j]
        pr = u % 2
        bGA, bBA = pr, 2 + pr

        def mm(e):
            ins = None
            if with_ga:
                for k in range(8):
                    ins = e.matmul(bank(bGA), lhsT=wblk[:, k, :], rhs=hT[:, k, CH(q)], start=(k == 0), stop=(k == 7))
            for k in range(8):
                ins = e.matmul(bank(bBA), lhsT=wblk[:, 8 + k, :], rhs=yT[:, k, CH(q)], start=(k == 0), stop=(k == 7))
            return ins
        rd = [("ring", slot)] + [("yT", k, q) for k in range(8)]
        wr = [PB(bBA)]
        if with_ga:
            rd += [("hT", k, q) for k in range(8)]
            wr.append(PB(bGA))
        P.add("pe", mm, reads=rd, writes=wr)
        P.add("act", lambda e: e.activation(out=tga[pr], in_=bank(bGA), func=AF.Tanh, scale=0.5,
                                            bias=dv[:, D_HBG + j:D_HBG + j + 1]),
              reads=[PB(bGA), ("dv",)], writes=[("tga", pr)])
        P.add("dve", lambda e: e.scalar_tensor_tensor(out=mergedT[:, j, CH(q)], in0=tga[pr], scalar=1.0, in1=bank(bBA),
                                                      op0=ALU.add, op1=ALU.mult),
              reads=[("tga", pr), PB(bBA)], writes=[("mergedT", j, q)])

    C_mm(0)
    D1_load(0)
    D1_ga(0)
    D1_ga(1)
    if stop == "B":
        return finish()
    P.fence(lambda e: e.memset(fsc[:, 0:1], 0.0),
            ["bufA", "bufI", "gel", "bufM", "xc", "hh", "xlb", "xlbpad", "xcb"],
            ["xp", "sA", "sB", "xppad", "tgb", "m2", "ypT", "mergedT"])
    P.fence(lambda e: e.memset(fsc[:, 1:2], 0.0), ["convd", "wabd", "wxbd"], ["tga", "pbf"])

    def padz(e):
        e.memset(xp1[:, 0:16], 0.0)
        e.memset(sA[:, 0:16], 0.0)
        return e.memset(sB[:, 0:16], 0.0)
    P.add("dve", padz, writes=[("xppad",)])
    P.fence(lambda e: e.memset(fsc[:, 3:4], 0.0), ["xinA", "hbA"], ["wo"])
    for hk in range(2):
        P.add("pool", lambda e, hk=hk: e.dma_start(
            out=woS[:, hk * 4:(hk + 1) * 4, :],
            in_=WO[:, hk * 4096:(hk + 1) * 4096].rearrange("p (a b) -> p a b", b=D)),
            writes=[("wo", hk)], dma="wo%d" % hk)

    C_evac(0)
    pend = C_pool_ops(0)
    sched = {7: ("y", 0, 1), 14: ("y", 1, 2), 20: ("y", 2, 3), 25: ("y", 3, None)}
    for u in range(32):
        j, q = u // 4, u % 4
        if q == 0:
            D1_load(j)
            D1_load(j + 1)
        if u >= 2:
            D1_ga(u)
        D1_rest(u)
        for _ in range(2):
            if pend:
                pend.pop(0)()
        if u in sched:
            _, iy, inext = sched[u]
            while pend:
                pend.pop(0)()
            C_y(iy)
            if inext is not None:
                C_mm(inext)
                C_evac(inext)
                pend = C_pool_ops(inext)
    assert not pend

    P.fence(lambda e: e.memset(fsc[:, 2:3], 0.0), ["poolw", "tga", "pbf", "pblk"], ["xinE"])
    mrgB = {}

    def D2_load(j):
        nonlocal rc
        if j < 8 and j not in mrgB:
            slot = rc % 2
            rc += 1
            load_ring(slot, WMRGB[j], 1536, None)
            mrgB[j] = (slot, ring[slot][:, 0:1536].rearrange("p (k n) -> p k n", n=128))

    for u in range(32):
        j, q = u // 4, u % 4
        if q == 0:
            D2_load(j)
            D2_load(j + 1)
        slot, wblk = mrgB[j]
        pr = u % 2
        bGB, bBB = 4 + pr, 6 + pr

        def mm(e, wblk=wblk, q=q, bGB=bGB, bBB=bBB):
            ins = None
            for k in range(8):
                ins = e.matmul(bank(bGB), lhsT=wblk[:, k, :], rhs=hT[:, k, CH(q)], start=(k == 0), stop=(k == 7))
            for k in range(4):
                ins = e.matmul(bank(bBB), lhsT=wblk[:, 8 + k, :], rhs=ypT[:, k, CH(q)], start=(k == 0), stop=(k == 3))
            return ins
        P.add("pe", mm, reads=[("ring", slot)] + [("hT", k, q) for k in range(8)] + [("ypT", k, q) for k in range(4)],
              writes=[PB(bGB), PB(bBB)])
        P.add("act", lambda e, j=j, pr=pr, bGB=bGB: e.activation(out=tgb[pr], in_=bank(bGB), func=AF.Tanh, scale=0.5,
                                                                bias=dv[:, D_HBG + 8 + j:D_HBG + 9 + j]),
              reads=[PB(bGB), ("dv",)], writes=[("tgb", pr)])
        P.add("dve", lambda e, pr=pr, bBB=bBB: e.scalar_tensor_tensor(out=m2b, in0=tgb[pr], scalar=1.0, in1=bank(bBB),
                                                                     op0=ALU.add, op1=ALU.mult),
              reads=[("tgb", pr), PB(bBB)], writes=[("m2",)])
        P.add("dve", lambda e, j=j, q=q: e.tensor_tensor(out=mergedT[:, j, CH(q)], in0=m2b, in1=mergedT[:, j, CH(q)],
                                                        op=ALU.add),
              reads=[("m2",), ("mergedT", j, q)], writes=[("mergedT", j, q)])

    if stop == "D":
        return finish()
    P.fence(lambda e: e.memset(fsc[:, 4:5], 0.0), ["yT", "xp", "sA", "sB", "xppad", "tgb", "m2"], ["wff2", "f1rx"])
    wff2S = carve(0, 64 * K, BF16, "p (a b) -> p a b", b=D)
    def wkey(kg):
        return ("wff2c", kg) if kg == 7 else ("wff2", kg)

    def load_wff2(kg):
        P.add("pool", lambda e: e.dma_start(
            out=wff2S[:, kg * 4:(kg + 1) * 4, :],
            in_=WFF2[:, kg * 4096:(kg + 1) * 4096].rearrange("p (a b) -> p a b", b=D)),
            writes=[wkey(kg)], dma="wff2_%d" % kg)
    xinE = [carve(144 * K + s * 4 * K, 4 * K) for s in range(4)] + [carve(112 * K + s * 4 * K, 4 * K) for s in range(2)]
    tmpE = carve(120 * K, 4 * K)
    h2b = [carve(124 * K + s * 2 * K, 2 * K, BF16) for s in range(2)]
    NXE = 6

    def xkeyE(s):
        return ("xinE", s) if s < 4 else ("xinE2", s)

    def E_load(tt):
        s = tt % NXE
        P.add("sp", lambda e: e.dma_start(out=xinE[s], in_=X[tt * 128:(tt + 1) * 128, :]),
              writes=[xkeyE(s)], dma="xinE%d" % s)

    def E_mm(tt):
        pi = tt % 3

        def mm_o(e):
            ins = None
            for nh in range(2):
                for k in range(8):
                    ins = e.matmul(bank(2 * pi + nh), lhsT=mergedT[:, k, tt * 128:(tt + 1) * 128],
                                   rhs=woS[:, k, nh * 512:(nh + 1) * 512], start=(k == 0), stop=(k == 7))
            return ins
        P.add("pe", mm_o, reads=[("wo", 0), ("wo", 1)] + [("mergedT", k, tt // 4) for k in range(8)],
              writes=[PB(2 * pi), PB(2 * pi + 1)])

    def E_ss1(tt):
        pi = tt % 3
        P.add("act", lambda e: e.activation(out=junk[:], in_=pps[pi][:], func=AF.Square,
                                            accum_out=st[:, 48 + tt:48 + tt + 1]),
              reads=[PB(2 * pi), PB(2 * pi + 1)], writes=[("ss", 1, tt)])
        rstd_act(1, tt, D_EPS4, ("ss", 1, tt))

    def E_x1(tt):
        s = tt % NXE
        xin = xinE[s]
        pi = tt % 3
        rs = rstd_dve(1, tt)
        for nh in range(2):
            P.add("dve", lambda e, nh=nh: e.scalar_tensor_tensor(
                out=tmpE[:, nh * 512:(nh + 1) * 512], in0=bank(2 * pi + nh), scalar=rs,
                in1=gB[:, nh * 512:(nh + 1) * 512], op0=ALU.mult, op1=ALU.mult),
                reads=[PB(2 * pi + nh), ("rs", 1, tt), ("gB",)], writes=[("tmpE2", nh)])
        P.add("dve", lambda e: e.tensor_tensor(out=xin, in0=tmpE, in1=xin, op=ALU.add),
              reads=[("tmpE2", 0), ("tmpE2", 1), xkeyE(s)], writes=[xkeyE(s)])

    def E_ss2(tt):
        s = tt % NXE
        xin = xinE[s]
        P.add("act", lambda e: e.activation(out=junk[:], in_=xin, func=AF.Square, accum_out=st[:, 96 + tt:96 + tt + 1]),
              reads=[xkeyE(s)], writes=[("ss", 2, tt)])
        P.add("act", lambda e: e.dma_start(out=X1S[tt * 128:(tt + 1) * 128, :], in_=xin),
              reads=[xkeyE(s)], writes=[("x1s", tt)], dma="x1w%d" % s)
        rstd_act(2, tt, D_EPS1, ("ss", 2, tt))

    def E_h2(tt):
        s = tt % NXE
        xin = xinE[s]
        rs2 = rstd_dve(2, tt)
        hb = h2b[tt % 2]
        P.add("dve", lambda e: e.scalar_tensor_tensor(out=hb, in0=xin, scalar=rs2, in1=gA[:], op0=ALU.mult,
                                                      op1=ALU.mult),
              reads=[xkeyE(s), ("rs", 2, tt), ("gA",)], writes=[("h2b", tt % 2)])

    def E_tr(tt):
        transposes_to_hT(h2b[tt % 2], ("h2b", tt % 2), tt, 6 + tt % 2, act_only=True, part="pe")

    def E_trev(tt):
        transposes_to_hT(None, None, tt, 6 + tt % 2, act_only=True, part="evac")

    P.fence(lambda e: e.memset(fsc[:, 7:8], 0.0), ["ring"], ["xinE2", "tmpE2", "h2b"])
    for tt0 in range(2):
        E_load(tt0)

    hid = carve(64 * K, 64 * K, BF16, "p (a b) -> p a b", b=1024)
    NR = 4
    f1r = [carve(128 * K + s * 2 * K, 2 * K, BF16, "p (a b) -> p a b", b=128) for s in range(NR)]
    sqb = [carve(136 * K + s * 2 * K, 2 * K) for s in range(2)]
    x1in = [carve(140 * K + s * 4 * K, 4 * K) for s in range(2)]
    outst = [carve(148 * K + s * 4 * K, 4 * K) for s in range(2)]
    fstate = {"fu": 2, "blk": 0}

    def hkey(m, q2):
        return ("hidA" if m < 24 else "hidB", m, q2)

    f1rx = [carve(56 * K + s * 2 * K, 2 * K, BF16, "p (a b) -> p a b", b=128) for s in range(NR)]

    def f1sel(b):
        if b < NR:
            return f1rx[b], ("f1rx", b), "f1rx%d" % b
        slot = b % NR
        return f1r[slot], ("f1r", slot), "f1r%d" % slot

    def ff1_load(b):
        m = b % 32
        dst, key, dname = f1sel(b)
        P.add("pool", lambda e: e.dma_start(out=dst, in_=WFF1[m].rearrange("p (a b) -> p a b", b=128)),
              writes=[key], dma=dname)

    def ff1_block():
        b = fstate["blk"]
        fstate["blk"] += 1
        th, m = b // 32, b % 32
        wts, wk, _ = f1sel(b)
        if b >= fstate.get("preloaded", 0):
            ff1_load(b)
        if b == 22:
            P.fence(lambda e: e.memset(fsc[:, 7:8], 0.0), ["f1rx"], ["wff2c"], eng="pool")
        if b in (4, 10, 16, 22):
            load_wff2(4 + (b - 4) // 6)
        for q2 in range(2):
            bk = fstate["fu"] % 4
            sqs = fstate["fu"] % 2
            fstate["fu"] += 1
            q = th * 2 + q2

            def mm_f1(e, q=q, bk=bk):
                ins = None
                for k in range(8):
                    ins = e.matmul(bank(bk), lhsT=wts[:, k, :], rhs=hT[:, k, CH(q)], start=(k == 0), stop=(k == 7))
                return ins
            P.add("pe", mm_f1, reads=[wk] + [("hT", k, q) for k in range(8)], writes=[PB(bk)])
            P.add("act", lambda e, bk=bk, sqs=sqs: e.activation(out=sqb[sqs], in_=bank(bk), func=AF.Square),
                  reads=[PB(bk)], writes=[("sq", sqs)])
            P.add("dve", lambda e, q2=q2, bk=bk, sqs=sqs: e.scalar_tensor_tensor(
                out=hid[:, m, q2 * 512:(q2 + 1) * 512], in0=bank(bk), scalar=0.0, in1=sqb[sqs],
                op0=ALU.is_gt, op1=ALU.mult),
                reads=[PB(bk), ("sq", sqs)], writes=[hkey(m, q2)])

    def ok(t):
        return 0 <= t < NTT
    if stop is None:
        for b in range(NR):
            ff1_load(b)
        fstate["preloaded"] = NR
    for sidx in range(NTT + 6):
        if ok(sidx - 6):
            E_trev(sidx - 6)
        if sidx % 4 == 0 and sidx // 4 < 4:
            load_wff2(sidx // 4)
        if ok(sidx - 5):
            E_tr(sidx - 5)
        if ok(sidx):
            E_mm(sidx)
        if sidx == NTT - 1:
            P.fence(lambda e: e.memset(fsc[:, 5:6], 0.0), ["mergedT", "ypT", "wo"], ["hidA", "f1r", "sq"], eng="pool")
        if ok(sidx - 1):
            E_ss1(sidx - 1)
        if ok(sidx - 3):
            E_ss2(sidx - 3)
        if ok(sidx - 2):
            E_x1(sidx - 2)
        if ok(sidx - 4):
            E_h2(sidx - 4)
        if ok(sidx + 2):
            E_load(sidx + 2)
        if sidx >= NTT and stop is None:
            ff1_block()
            if sidx > NTT:
                ff1_block()

    if stop == "E":
        return finish()
    P.add("sp", lambda e: e.dma_start(out=gB[:], in_=GAINS[3:4, :].partition_broadcast(128)),
          writes=[("gB",)], dma="gB")
    P.fence(lambda e: e.memset(fsc[:, 6:7], 0.0),
            ["ring", "xinE", "tmpE", "h2b", "xinE2", "tmpE2"], ["hidB", "x1in", "outst"])
    for th in range(2):
        while fstate["blk"] < (th + 1) * 32:
            ff1_block()
        if stop == "F1" and th == 0:
            return finish()
        for tl in range(8):
            tt = th * 8 + tl
            s = tt % 2
            P.add("sp", lambda e, s=s, tt=tt: e.dma_start(out=x1in[s], in_=X1S[tt * 128:(tt + 1) * 128, :]),
                  reads=[("x1s", tt)], writes=[("x1in", s)], dma="x1in%d" % s)
            pi = 2 + tt % 2

            def mm_f2(e, tl=tl, pi=pi):
                ins = None
                for nh in range(2):
                    for k in range(32):
                        ins = e.matmul(bank(2 * pi + nh), lhsT=hid[:, k, tl * 128:(tl + 1) * 128],
                                       rhs=wff2S[:, k, nh * 512:(nh + 1) * 512], start=(k == 0), stop=(k == 31))
                return ins
            P.add("pe", mm_f2, reads=[wkey(kg) for kg in range(8)] + [hkey(k, tl // 4) for k in range(32)],
                  writes=[PB(2 * pi), PB(2 * pi + 1)])
            P.add("act", lambda e, tt=tt, pi=pi: e.activation(out=junk[:], in_=pps[pi][:], func=AF.Square,
                                                             accum_out=st[:, 144 + tt:144 + tt + 1]),
                  reads=[PB(2 * pi), PB(2 * pi + 1)], writes=[("ss", 3, tt)])
            rs = rstd_ops(3, tt, D_EPS1, ("ss", 3, tt))
            for nh in range(2):
                P.add("dve", lambda e, pi=pi, rs=rs, s=s, nh=nh: e.scalar_tensor_tensor(
                    out=outst[s][:, nh * 512:(nh + 1) * 512], in0=bank(2 * pi + nh), scalar=rs,
                    in1=gB[:, nh * 512:(nh + 1) * 512], op0=ALU.mult, op1=ALU.mult),
                    reads=[PB(2 * pi + nh), ("rs", 3, tt), ("gB",)], writes=[("outst", s, nh)])
            for nh in range(2):
                hs = slice(nh * 512, (nh + 1) * 512)
                P.add("dve", lambda e, s=s, hs=hs: e.tensor_tensor(out=outst[s][:, hs], in0=outst[s][:, hs],
                                                                   in1=x1in[s][:, hs], op=ALU.add),
                      reads=[("outst", s, nh), ("x1in", s)], writes=[("outst", s, nh)])
                P.add("sp", lambda e, s=s, tt=tt, hs=hs: e.dma_start(out=OUT[tt * 128:(tt + 1) * 128, hs],
                                                                    in_=outst[s][:, hs]),
                      reads=[("outst", s, nh)], dma="out%d_%d" % (s, nh))
        if stop == "F2" and th == 0:
            return finish()

    return finish()


def _colvec(v):
    v = np.asarray(v, np.float32).reshape(-1, 128)
    return np.ascontiguousarray(v.T)


def _prep_shared(inp):
    f = lambda a: np.ascontiguousarray(np.asarray(a, np.float32))
    w_in = f(inp["w_in"])[0]
    w_lru_up = f(inp["w_lru_up"])[0]
    w_pool_up = f(inp["w_pool_up"])[0]
    w_o = f(inp["w_o"])[0]
    w_ff1 = f(inp["w_ff1"])[0]
    w_ff2 = f(inp["w_ff2"])[0]

    def ktile(w):
        kk = w.shape[0] // 128
        return w.reshape(kk, 128, w.shape[1]).transpose(1, 0, 2)

    wlru = np.empty((8, 128, 8, 256), np.float32)
    for c in range(8):
        wlru[c, :, :, 0:128] = ktile(w_in[:, c * 128:(c + 1) * 128])
        wlru[c, :, :, 128:256] = ktile(w_in[:, 1024 + c * 128:1024 + (c + 1) * 128])
    wpool = np.empty((4, 128, 8, 128), np.float32)
    for g in range(4):
        wpool[g] = ktile(w_in[:, 2048 + g * 128:2048 + (g + 1) * 128])
    wmrga = np.empty((8, 128, 16, 128), np.float32)
    wmrgb = np.empty((8, 128, 12, 128), np.float32)
    for j in range(8):
        wmrga[j, :, 0:8] = ktile(w_in[:, 2560 + j * 128:2560 + (j + 1) * 128])
        wmrga[j, :, 8:16] = ktile(w_lru_up[:, j * 128:(j + 1) * 128])
        wmrgb[j, :, 0:8] = ktile(w_in[:, 3584 + j * 128:3584 + (j + 1) * 128])
        wmrgb[j, :, 8:12] = ktile(w_pool_up[:, j * 128:(j + 1) * 128])
    wff1 = np.empty((32, 128, 8, 128), np.float32)
    for m in range(32):
        wff1[m] = ktile(w_ff1[:, m * 128:(m + 1) * 128])

    def blockdiag(w):
        out = np.zeros((128, 8, 128), np.float32)
        for h in range(16):
            c, o = h // 2, (h % 2) * 64
            out[o:o + 64, c, o:o + 64] = w[h]
        return out
    wabd = blockdiag(f(inp["lru_w_a"])[0])
    wxbd = blockdiag(f(inp["lru_w_x"])[0])
    conv_w = f(inp["conv_w"])[0]
    convd = np.zeros((128, 32, 128), np.float32)
    idx = np.arange(128)
    for c in range(8):
        for k4 in range(4):
            convd[idx, c * 4 + k4, idx] = conv_w[k4, c * 128:(c + 1) * 128]
    poolw = f(inp["pool_w"])[0].transpose(1, 0, 2)
    colvecs = np.concatenate([
        _colvec(inp["conv_b"]), _colvec(inp["lru_b_a"]), _colvec(inp["lru_b_x"]), _colvec(inp["lru_lambda"]),
        _colvec(inp["pool_scale"]), _colvec(inp["b_gate"])], axis=1)
    gains = np.stack([f(inp["norm_mix_pre"])[0], f(inp["norm_mix_post"])[0],
                      f(inp["norm_mlp_pre"])[0], f(inp["norm_mlp_post"])[0]], 0)
    return {
        "wlru": np.ascontiguousarray(wlru.reshape(8, 128, 2048)),
        "wpool": np.ascontiguousarray(wpool.reshape(4, 128, 1024)),
        "wmrga": np.ascontiguousarray(wmrga.reshape(8, 128, 2048)),
        "wmrgb": np.ascontiguousarray(wmrgb.reshape(8, 128, 1536)),
        "wo": np.ascontiguousarray(ktile(w_o).reshape(128, 8192)),
        "wff1": np.ascontiguousarray(wff1.reshape(32, 128, 1024)),
        "wff2": np.ascontiguousarray(ktile(w_ff2).reshape(128, 32768)),
        "wabd": np.ascontiguousarray(wabd.reshape(128, 1024)),
        "wxbd": np.ascontiguousarray(wxbd.reshape(128, 1024)),
        "convd": np.ascontiguousarray(convd.reshape(128, 4096)),
        "poolw": np.ascontiguousarray(poolw.reshape(128, 512)),
        "ident": np.eye(128, dtype=np.float32),
        "colvecs": np.ascontiguousarray(colvecs.astype(np.float32)),
        "gains": np.ascontiguousarray(gains),
    }


_NC_CACHE = {}


def kernel(**inputs):
    x = np.ascontiguousarray(np.asarray(inputs["x"], np.float32))
    shared = _prep_shared(inputs)
    if "nc" not in _NC_CACHE:
        _NC_CACHE["nc"] = build_nc()
    nc = _NC_CACHE["nc"]
    in_maps = []
    for b in range(NCORES):
        m = dict(shared)
        m["x"] = x[b]
        in_maps.append(m)
    res = run_bass_kernel_spmd(nc, in_maps, core_ids=list(range(NCORES)))
    out = np.stack([np.asarray(r["out"], np.float32) for r in res.results], 0)
    return out
```

```python
import numpy as np
from contextlib import ExitStack
import concourse.bass as bass
import concourse.mybir as mybir
from concourse.bass_utils import run_bass_kernel_spmd

F32 = mybir.dt.float32
BF16 = mybir.dt.bfloat16
AF = mybir.ActivationFunctionType
ALU = mybir.AluOpType

T = 2048
D = 1024
NCORES = 8
EPS = 1e-6
K = 1024


class Prog:
    ENGS = ("sp", "act", "pool", "dve", "pe")
    MAX_SWDGE = 4

    def __init__(self, nc, es):
        self.nc = nc
        self.es = es
        self.ops = []
        self.lw = {}
        self.rd = {}
        self.dma_names = []
        self.floors = {}
        self.pool_dmas = []

    def add(self, eng, fn, reads=(), writes=(), dma=None):
        deps = set()
        for k in list(reads) + list(writes):
            f = self.floors.get(k[0])
            if f is not None:
                deps.add(f)
        for k in reads:
            if k in self.lw:
                deps.add(self.lw[k])
        for k in writes:
            if k in self.lw:
                deps.add(self.lw[k])
            deps |= self.rd.get(k, set())
        idx = len(self.ops)
        if dma is not None and eng == "pool":
            if len(self.pool_dmas) >= self.MAX_SWDGE:
                deps.add(self.pool_dmas[-self.MAX_SWDGE])
            self.pool_dmas.append(idx)
        self.ops.append(dict(eng=eng, fn=fn, deps=deps, dma=dma, sig=False))
        if dma is not None and dma not in self.dma_names:
            self.dma_names.append(dma)
        for k in reads:
            self.rd.setdefault(k, set()).add(idx)
        for k in writes:
            self.lw[k] = idx
            self.rd[k] = set()
        return idx

    def fence(self, fn, old_names, new_names, eng="dve"):
        deps = set()
        old = set(old_names)
        for k, i in self.lw.items():
            if k[0] in old:
                deps.add(i)
        for k, s in self.rd.items():
            if k[0] in old:
                deps |= s
        for n in old:
            f = self.floors.get(n)
            if f is not None:
                deps.add(f)
        idx = len(self.ops)
        self.ops.append(dict(eng=eng, fn=fn, deps=deps, dma=None, sig=False))
        for n in new_names:
            self.floors[n] = idx
        return idx

    def emit(self, final_wait=()):
        nc, es = self.nc, self.es
        ops = self.ops

        def skip(po, c):
            return po["dma"] is None and c["dma"] is None and po["eng"] == "pe" and c["eng"] == "pe"

        for c in ops:
            for p in c["deps"]:
                if not skip(ops[p], c):
                    ops[p]["sig"] = True
        esem = {e: es.enter_context(nc.semaphore("s_" + e)) for e in self.ENGS}
        dsem = {n: es.enter_context(nc.semaphore("d_" + str(n))) for n in self.dma_names}
        cnt = {e: 0 for e in self.ENGS}
        dcnt = {n: 0 for n in self.dma_names}
        for o in ops:
            if o["dma"] is not None:
                dcnt[o["dma"]] += 16
                o["sem"], o["val"] = dsem[o["dma"]], dcnt[o["dma"]]
            elif o["sig"]:
                cnt[o["eng"]] += 1
                o["sem"], o["val"] = esem[o["eng"]], cnt[o["eng"]]
        per = {e: [o for o in ops if o["eng"] == e] for e in self.ENGS}

        def run(e, eng):
            waited = {}
            for o in per[e]:
                need = {}
                for p in o["deps"]:
                    po = ops[p]
                    if "sem" not in po or skip(po, o):
                        continue
                    s = po["sem"]
                    if po["val"] > need.get(id(s), (None, 0))[1]:
                        need[id(s)] = (s, po["val"])
                for key, (s, v) in need.items():
                    if waited.get(key, 0) >= v:
                        continue
                    eng.wait_ge(s, v)
                    waited[key] = v
                ins = o["fn"](eng)
                if "sem" in o:
                    ins.then_inc(o["sem"], 16 if o["dma"] is not None else 1)
            if e == "sp":
                for n in final_wait:
                    eng.wait_ge(dsem[n], dcnt[n])

        with nc.Block() as block:
            @block.sync
            def _(eng):
                run("sp", eng)

            @block.scalar
            def _(eng):
                run("act", eng)

            @block.gpsimd
            def _(eng):
                run("pool", eng)

            @block.vector
            def _(eng):
                run("dve", eng)

            @block.tensor
            def _(eng):
                run("pe", eng)


def build_nc(stop=None):
    nc = bass.Bass("TRN2", target_bir_lowering=False)
    es = ExitStack()

    def dram(name, shape, kind="ExternalInput"):
        return nc.dram_tensor(name, shape, F32, kind=kind).ap()

    X = dram("x", [T, D])
    OUT = dram("out", [T, D], "ExternalOutput")
    WLRU = dram("wlru", [8, 128, 2048])
    WPOOL = dram("wpool", [4, 128, 1024])
    WMRGA = dram("wmrga", [8, 128, 2048])
    WMRGB = dram("wmrgb", [8, 128, 1536])
    WO = dram("wo", [128, 8192])
    WFF1 = dram("wff1", [32, 128, 1024])
    WFF2 = dram("wff2", [128, 32768])
    WABD = dram("wabd", [128, 1024])
    WXBD = dram("wxbd", [128, 1024])
    CONVD = dram("convd", [128, 4096])
    POOLW = dram("poolw", [128, 512])
    IDENT = dram("ident", [128, 128])
    CVEC = dram("colvecs", [128, 52])
    GAINS = dram("gains", [4, D])
    X1S = dram("x1s", [T, D], "Internal")

    def sb(name, shape, dt):
        return es.enter_context(nc.sbuf_tensor(name, shape, dt))

    cvt = sb("cvt", [128, 52], F32)
    dv = sb("dv", [128, 64], F32)
    st = sb("st", [128, 4 * 48], F32)
    icnt = sb("icnt", [128, 16], F32)
    gA = sb("gA", [128, D], F32)
    gB = sb("gB", [128, D], F32)
    identb = sb("identb", [128, 128], BF16)
    junk = sb("junk", [128, D], BF16)
    fsc = sb("fsc", [128, 8], F32)
    hT = sb("hT", [128, 8, T], BF16)
    BIGN = 160 * 256
    big = sb("big", [128, BIGN], F32)
    pps = [es.enter_context(nc.psum_tensor("pp%d" % i, [128, 1024], F32)) for i in range(4)]

    def bank(i):
        return pps[i // 2][:, (i % 2) * 512:(i % 2) * 512 + 512]

    def bank_bf(i):
        return bank(i).bitcast(BF16)

    def carve(off, nbytes, dt=F32, pat=None, **kw):
        assert off % 4 == 0 and nbytes % 4 == 0 and off + nbytes <= BIGN * 4, (off, nbytes)
        ap = big[:, off // 4:(off + nbytes) // 4]
        if dt == BF16:
            ap = ap.bitcast(BF16)
        if pat is not None:
            ap = ap.rearrange(pat, **kw)
        return ap

    C_CONVB, C_BA, C_BX, C_LAM, C_PSC, C_BG = 0, 8, 16, 24, 32, 36
    D_HBA, D_HBX, D_HBG, D_CH, D_T0, D_EPS1, D_EPS4 = 0, 8, 16, 32, 40, 48, 49

    P = Prog(nc, es)
    warm = sb("warm", [128, 8], F32)

    def act_warm(func, col):
        P.add("act", lambda e: e.activation(out=warm[:, col + 1:col + 2], in_=warm[:, 0:1], func=func),
              reads=[("warm",)], writes=[("warmo", col)])

    def finish():
        P.emit(final_wait=[n for n in ("out0_0", "out0_1", "out1_0", "out1_1") if n in P.dma_names])
        return nc

    def PB(i):
        return ("pb", i)

    yT = carve(0, 32 * K, BF16, "p (a b) -> p a b", b=T)
    mergedT = carve(64 * K, 32 * K, BF16, "p (a b) -> p a b", b=T)
    ypT = carve(96 * K, 16 * K, BF16, "p (a b) -> p a b", b=T)
    ring = [carve(112 * K + s * 8 * K, 8 * K, BF16) for s in range(2)]
    woS = carve(128 * K, 16 * K, BF16, "p (a b) -> p a b", b=D)
    wabd = carve(144 * K, 2 * K, BF16, "p (a b) -> p a b", b=128)
    wxbd = carve(146 * K, 2 * K, BF16, "p (a b) -> p a b", b=128)
    convd = carve(148 * K, 8 * K, BF16, "p (a b) -> p a b", b=128)
    poolw = carve(156 * K, 1 * K, BF16, "p (a b) -> p a b", b=128)
    xinA = [carve(128 * K + s * 4 * K, 4 * K) for s in range(3)]
    hbA = [carve(140 * K + s * 2 * K, 2 * K, BF16) for s in range(2)]

    P.add("dve", lambda e: e.memset(warm[:, 0:1], 1.0), writes=[("warm",)])
    act_warm(AF.Sqrt, 0)
    xin_first = [carve(128 * K + s * 4 * K, 4 * K) for s in range(2)]
    for tt0 in range(2):
        P.add("sp", lambda e, tt0=tt0: e.dma_start(out=xin_first[tt0], in_=X[tt0 * 128:(tt0 + 1) * 128, :]),
              writes=[("xinA", tt0)], dma="xinA%d" % tt0)
    P.add("sp", lambda e: e.dma_start(out=cvt[:], in_=CVEC), writes=[("cvt",)], dma="cv")
    P.add("sp", lambda e: e.dma_start(out=gA[:], in_=GAINS[0:1, :].partition_broadcast(128)),
          writes=[("gA",)], dma="gA")
    P.add("pool", lambda e: e.dma_start(out=identb[:], in_=IDENT), writes=[("ident",)], dma="ident")

    def late_const_loads(after_key):
        P.add("pool", lambda e: e.dma_start(out=convd, in_=CONVD.rearrange("p (a b) -> p a b", b=128)),
              reads=[after_key], writes=[("convd",)], dma="convd")
        P.add("pool", lambda e: e.dma_start(out=wabd, in_=WABD.rearrange("p (a b) -> p a b", b=128)),
              writes=[("wabd",)], dma="wabd")
        P.add("pool", lambda e: e.dma_start(out=wxbd, in_=WXBD.rearrange("p (a b) -> p a b", b=128)),
              writes=[("wxbd",)], dma="wxbd")
        P.add("pool", lambda e: e.dma_start(out=poolw, in_=POOLW.rearrange("p (a b) -> p a b", b=128)),
              writes=[("poolw",)], dma="poolw")

    def setup_dv(e):
        e.memset(dv[:, D_EPS1:D_EPS1 + 1], EPS)
        e.memset(dv[:, D_EPS4:D_EPS4 + 1], 4.0 * EPS)
        for t in range(16):
            e.memset(icnt[:, t:t + 1], 1.0 / (t + 1))
        e.tensor_scalar(out=dv[:, D_HBA:D_HBA + 16], in0=cvt[:, C_BA:C_BA + 16], scalar1=0.5, scalar2=None,
                        op0=ALU.mult)
        return e.tensor_scalar(out=dv[:, D_HBG:D_HBG + 16], in0=cvt[:, C_BG:C_BG + 16], scalar1=0.5,
                               scalar2=None, op0=ALU.mult)
    P.add("dve", setup_dv, reads=[("cvt",)], writes=[("dv",)])
    def softplus_setup():
        P.add("act", lambda e: e.activation(out=dv[:, D_T0:D_T0 + 8], in_=cvt[:, C_LAM:C_LAM + 8], func=AF.Exp,
                                            scale=-1.0), reads=[("cvt",)], writes=[("dvt",)])
        P.add("act", lambda e: e.activation(out=dv[:, D_T0:D_T0 + 8], in_=dv[:, D_T0:D_T0 + 8], func=AF.Ln,
                                            bias=1.0), reads=[("dvt",)], writes=[("dvt",)])
        P.add("dve", lambda e: e.tensor_scalar(out=dv[:, D_CH:D_CH + 8], in0=dv[:, D_T0:D_T0 + 8], scalar1=-4.0,
                                               scalar2=None, op0=ALU.mult), reads=[("dvt",)], writes=[("dvc",)])

    def rstd_act(n, col, eps_col, ss_key):
        base = n * 48
        P.add("act", lambda e: e.activation(out=st[:, base + 16 + col:base + 17 + col],
                                            in_=st[:, base + col:base + col + 1], func=AF.Sqrt,
                                            scale=1.0 / D, bias=dv[:, eps_col:eps_col + 1]),
              reads=[ss_key, ("dv",)], writes=[("sd", n, col)])

    def rstd_dve(n, col):
        base = n * 48
        P.add("dve", lambda e: e.reciprocal(out=st[:, base + 32 + col:base + 33 + col],
                                            in_=st[:, base + 16 + col:base + 17 + col]),
              reads=[("sd", n, col)], writes=[("rs", n, col)])
        return st[:, base + 32 + col:base + 33 + col]

    def rstd_ops(n, col, eps_col, ss_key):
        rstd_act(n, col, eps_col, ss_key)
        return rstd_dve(n, col)

    def transposes_to_hT(src_ap, src_key, tt, bk, act_only=False, part="both"):
        if part in ("both", "pe"):
            transposes_pe(src_ap, src_key, bk)
        if part in ("both", "evac"):
            transposes_evac(tt, bk, act_only)

    def transposes_pe(src_ap, src_key, bk):
        def tr(e):
            ins = None
            for c in range(8):
                ins = e.transpose(out=bank_bf(bk)[:, c * 128:(c + 1) * 128], in_=src_ap[:, c * 128:(c + 1) * 128],
                                  identity=identb[:])
            return ins
        P.add("pe", tr, reads=[src_key, ("ident",)], writes=[PB(bk)])

    def transposes_evac(tt, bk, act_only):
        outap = hT[:, :, tt * 128:(tt + 1) * 128]
        inap = bank_bf(bk).rearrange("p (a b) -> p a b", b=128)
        wkeys = [("hT", c, tt // 4) for c in range(8)]
        if tt % 2 == 0 or act_only:
            P.add("act", lambda e: e.activation(out=outap, in_=inap, func=AF.Copy), reads=[PB(bk)], writes=wkeys)
        else:
            P.add("dve", lambda e: e.tensor_copy(out=outap, in_=inap), reads=[PB(bk)], writes=wkeys)

    NTT = T // 128
    WKB = 32 * K
    xinA6 = xinA + [carve(s * 4 * K, 4 * K) for s in range(3)]

    def xkeyA(s):
        return ("xinA", s) if s < 3 else ("xinA2", s)

    def A_stage1(tt):
        s = tt % 6
        xin = xinA6[s]
        if tt >= 2:
            P.add("sp", lambda e: e.dma_start(out=xin, in_=X[tt * 128:(tt + 1) * 128, :]),
                  writes=[xkeyA(s)], dma="xinA%d" % s)
        if tt == 4:
            late_const_loads(xkeyA(3))
        if tt == 6:
            softplus_setup()
        P.add("act", lambda e: e.activation(out=junk[:], in_=xin, func=AF.Square, accum_out=st[:, tt:tt + 1]),
              reads=[xkeyA(s)], writes=[("ss", 0, tt)])
        rstd_act(0, tt, D_EPS1, ("ss", 0, tt))

    def A_stage2(tt):
        s = tt % 6
        xin = xinA6[s]
        rs = rstd_dve(0, tt)
        hb = hbA[tt % 2]
        P.add("dve", lambda e: e.scalar_tensor_tensor(out=hb, in0=xin, scalar=rs, in1=gA[:], op0=ALU.mult,
                                                      op1=ALU.mult),
              reads=[xkeyA(s), ("rs", 0, tt), ("gA",)], writes=[("hbA", tt % 2)])
        transposes_to_hT(hb, ("hbA", tt % 2), tt, tt % 2, part="pe")

    def A_stage3(tt):
        transposes_to_hT(None, None, tt, tt % 2, part="evac")

    for sidx in range(NTT + 2):
        if sidx < NTT:
            A_stage1(sidx)
        if 1 <= sidx <= NTT:
            A_stage2(sidx - 1)
        if sidx >= 2:
            A_stage3(sidx - 2)

    if stop == "A":
        return finish()
    act_warm(AF.Gelu_apprx_tanh, 2)
    P.add("sp", lambda e: e.dma_start(out=gB[:], in_=GAINS[1:2, :].partition_broadcast(128)),
          writes=[("gB",)], dma="gB")
    P.add("sp", lambda e: e.dma_start(out=gA[:], in_=GAINS[2:3, :].partition_broadcast(128)),
          writes=[("gA",)], dma="gA")
    P.fence(lambda e: e.memset(fsc[:, 6:7], 0.0), ["xinA2"], ["yT"])

    bufA = [carve(WKB + p * 8 * K, 8 * K) for p in range(2)]
    bufI = [carve(WKB + 16 * K + p * 8 * K, 8 * K) for p in range(2)]
    gel = [carve(WKB + 32 * K + p * 8 * K, 8 * K) for p in range(3)]
    bufM = carve(WKB + 56 * K, 8 * K)
    xcr = [carve(WKB + 64 * K + p * 2 * K, 2 * K) for p in range(2)]
    hhT = bufM
    xlb = carve(WKB + 68 * K, 4608, BF16)
    xcb = [carve(WKB + 73 * K + p * K, K, BF16) for p in range(2)]

    P.add("dve", lambda e: e.memset(xlb[:, 0:4], 0.0), writes=[("xlbpad",)])

    def CH(q):
        return slice(q * 512, (q + 1) * 512)

    def load_ring(slot, src_ap, n_el, key):
        dst = ring[slot][:, 0:n_el]
        P.add("pool", lambda e: e.dma_start(out=dst, in_=src_ap), writes=[("ring", slot)], dma="ring%d" % slot)

    rc = 0
    lru_blk = {}

    def B_stage1(u):
        nonlocal rc
        c, q = u // 4, u % 4
        par = c % 2
        if q == 0:
            for cc in (c, c + 1):
                if cc < 8 and cc not in lru_blk:
                    slot = rc % 2
                    rc += 1
                    load_ring(slot, WLRU[cc], 2048, None)
                    lru_blk[cc] = (slot, ring[slot][:, 0:2048].rearrange("p (k n) -> p k n", n=256))
        slot, wblk = lru_blk[c]
        bXA, bGB = u % 2, 2 + u % 2

        def mm_x(e):
            ins = None
            for k in range(8):
                ins = e.matmul(bank(bXA), lhsT=wblk[:, k, 0:128], rhs=hT[:, k, CH(q)], start=(k == 0), stop=(k == 7))
            for k in range(8):
                ins = e.matmul(bank(bGB), lhsT=wblk[:, k, 128:256], rhs=hT[:, k, CH(q)], start=(k == 0), stop=(k == 7))
            return ins
        P.add("pe", mm_x, reads=[("ring", slot)] + [("hT", k, q) for k in range(8)], writes=[PB(bXA), PB(bGB)])
        P.add("dve", lambda e: e.tensor_copy(out=xlb[:, 3 + q * 512:3 + (q + 1) * 512], in_=bank(bXA)),
              reads=[PB(bXA), ("xlbpad",)], writes=[("xlb", q)])
        P.add("act", lambda e: e.activation(out=gel[c % 3][:, CH(q)], in_=bank(bGB), func=AF.Gelu_apprx_tanh),
              reads=[PB(bGB)], writes=[("gel", c % 3, q)])

    def B_stage2(u):
        c, q = u // 4, u % 4
        bCV = 4 + u % 2

        def mm_conv(e):
            ins = None
            for k4 in range(4):
                ins = e.matmul(bank(bCV), lhsT=convd[:, c * 4 + k4, :], rhs=xlb[:, q * 512 + k4:q * 512 + k4 + 512],
                               start=(k4 == 0), stop=(k4 == 3))
            return ins
        P.add("pe", mm_conv, reads=[("convd",), ("xlb", q)] + ([("xlb", q - 1)] if q > 0 else [("xlbpad",)]),
              writes=[PB(bCV)])
        xc = xcr[u % 2]
        P.add("dve", lambda e: e.tensor_scalar(out=xc, in0=bank(bCV), scalar1=cvt[:, C_CONVB + c:C_CONVB + c + 1],
                                               scalar2=None, op0=ALU.add),
              reads=[PB(bCV), ("cvt",)], writes=[("xc", u % 2)])
        P.add("dve", lambda e: e.tensor_copy(out=xcb[u % 2], in_=xc), reads=[("xc", u % 2)], writes=[("xcb", u % 2)])

    def B_stage3(u):
        c = u // 4

        def mm_g(e):
            e.matmul(bank(6), lhsT=wabd[:, c, :], rhs=xcb[u % 2], start=True, stop=True)
            return e.matmul(bank(7), lhsT=wxbd[:, c, :], rhs=xcb[u % 2], start=True, stop=True)
        P.add("pe", mm_g, reads=[("wabd",), ("wxbd",), ("xcb", u % 2)], writes=[PB(6), PB(7)])

    def B_stage4(u):
        c, q = u // 4, u % 4
        par = c % 2
        xc = xcr[u % 2]
        P.add("act", lambda e: e.activation(out=bufA[par][:, CH(q)], in_=bank(6), func=AF.Tanh, scale=0.5,
                                            bias=dv[:, D_HBA + c:D_HBA + c + 1]),
              reads=[PB(6), ("dv",)], writes=[("bufA", par, q)])
        P.add("act", lambda e: e.activation(out=bufI[par][:, CH(q)], in_=bank(7), func=AF.Tanh, scale=0.5,
                                            bias=dv[:, D_HBX + c:D_HBX + c + 1]),
              reads=[PB(7), ("dv",)], writes=[("bufI", par, q)])
        P.add("dve", lambda e: e.scalar_tensor_tensor(out=bufI[par][:, CH(q)], in0=bufI[par][:, CH(q)], scalar=1.0,
                                                      in1=xc, op0=ALU.add, op1=ALU.mult),
              reads=[("bufI", par, q), ("xc", u % 2)], writes=[("bufI", par, q)])

    def B_back1(c):
        par = c % 2
        for q in range(4):
            P.add("act", lambda e, q=q: e.activation(out=bufA[par][:, CH(q)], in_=bufA[par][:, CH(q)], func=AF.Exp,
                                                     scale=dv[:, D_CH + c:D_CH + c + 1], bias=dv[:, D_CH + c:D_CH + c + 1]),
                  reads=[("bufA", par, q), ("dvc",)], writes=[("bufA", par, q)])
        for q in range(4):
            P.add("act", lambda e, q=q: e.activation(out=bufM[:, CH(q)], in_=bufA[par][:, CH(q)], func=AF.Square),
                  reads=[("bufA", par, q)], writes=[("bufM", q)])

    def B_back2(c):
        par = c % 2
        for q in range(4):
            P.add("act", lambda e, q=q: e.activation(out=bufM[:, CH(q)], in_=bufM[:, CH(q)], func=AF.Sqrt, scale=-1.0,
                                                     bias=1.0),
                  reads=[("bufM", q)], writes=[("bufM", q)])
        for q in range(4):
            P.add("dve", lambda e, q=q: e.tensor_tensor(out=bufI[par][:, CH(q)], in0=bufI[par][:, CH(q)],
                                                        in1=bufM[:, CH(q)], op=ALU.mult),
                  reads=[("bufI", par, q), ("bufM", q)], writes=[("bufI", par, q)])

    def B_back3(c):
        par = c % 2
        P.add("dve", lambda e: e.tensor_tensor_scan(out=hhT, data0=bufA[par], data1=bufI[par], initial=0.0,
                                                    op0=ALU.mult, op1=ALU.add),
              reads=[("bufA", par, q) for q in range(4)] + [("bufI", par, q) for q in range(4)]
              + [("bufM", q) for q in range(4)], writes=[("bufM", q) for q in range(4)])
        P.add("dve", lambda e: e.scalar_tensor_tensor(out=yT[:, c, :], in0=hhT, scalar=0.5, in1=gel[c % 3],
                                                      op0=ALU.mult, op1=ALU.mult),
              reads=[("bufM", q) for q in range(4)] + [("gel", c % 3, q) for q in range(4)],
              writes=[("yT", c, q) for q in range(4)])

    NU = 32
    for sidx in range(NU + 8):
        u4 = sidx - 3
        if 0 <= u4 < NU:
            B_stage4(u4)
            if u4 % 4 == 3:
                B_back1(u4 // 4)
        u5 = sidx - 4
        if 0 <= u5 < NU and u5 % 4 == 3:
            B_back2(u5 // 4)
        u6 = sidx - 6
        if 0 <= u6 < NU and u6 % 4 == 3:
            B_back3(u6 // 4)
        if sidx < NU:
            B_stage1(sidx)
        if 0 <= sidx - 1 < NU:
            B_stage2(sidx - 1)
        if 0 <= sidx - 2 < NU:
            B_stage3(sidx - 2)

    WINS = (2, 4, 8, 16)
    GORD = (3, 2, 1, 0)
    pblk = carve(157 * K, 2 * K, BF16, "p (k n) -> p k n", n=128)
    tga = [carve(144 * K + p * 2 * K, 2 * K) for p in range(2)]
    pbfs = [carve(148 * K + i * 4 * K, 4 * K, BF16) for i in range(2)]
    XPW = 2064
    XPB = 8448
    xp1 = carve(WKB, XPW * 4)
    sA = carve(WKB + XPB, XPW * 4)
    sB = carve(WKB + 2 * XPB, XPW * 4)
    tgb = [carve(WKB + 3 * XPB + p * 2 * K, 2 * K) for p in range(2)]
    m2b = carve(WKB + 3 * XPB + 4 * K, 2 * K)

    def C_mm(i):
        g = GORD[i]
        P.add("pool", lambda e: e.dma_start(out=pblk, in_=WPOOL[g].rearrange("p (k n) -> p k n", n=128)),
              writes=[("pblk",)], dma="pblk")
        for q in range(4):
            bk = 4 + q

            def mm_p(e, q=q, bk=bk):
                ins = None
                for k in range(8):
                    ins = e.matmul(bank(bk), lhsT=pblk[:, k, :], rhs=hT[:, k, CH(q)], start=(k == 0), stop=(k == 7))
                return ins
            P.add("pe", mm_p, reads=[("pblk",)] + [("hT", k, q) for k in range(8)], writes=[PB(bk)])

    def C_evac(i):
        for q in range(4):
            bk = 4 + q
            P.add("act", lambda e, q=q, bk=bk: e.activation(out=xp1[:, 16 + q * 512:16 + (q + 1) * 512], in_=bank(bk),
                                                           func=AF.Copy),
                  reads=[PB(bk), ("xppad",)], writes=[("xp", q)])

    def C_pool_ops(i):
        g = GORD[i]
        w = WINS[g]
        pbf = pbfs[i % 2]
        xk = [("xp", q) for q in range(4)]
        ops = []
        src, srck = xp1, xk
        sh = 1
        bufs = [(sA, [("sA", q) for q in range(4)]), (sB, [("sB", q) for q in range(4)])]
        bi = 0
        while sh < w:
            dst, dstk = bufs[bi % 2]
            ops.append(lambda src=src, dst=dst, sh=sh, srck=srck, dstk=dstk: P.add(
                "dve", lambda e: e.tensor_tensor(out=dst[:, 16:16 + T], in0=src[:, 16:16 + T],
                                                 in1=src[:, 16 - sh:16 - sh + T], op=ALU.add),
                reads=list(srck) + [("xppad",)], writes=list(dstk)))
            src, srck = dst, dstk
            sh *= 2
            bi += 1
        ops.append(lambda src=src, srck=srck: P.add(
            "dve", lambda e: e.scalar_tensor_tensor(out=pbf, in0=src[:, 16:16 + T], scalar=1.0 / w,
                                                    in1=xp1[:, 16:16 + T], op0=ALU.mult, op1=ALU.subtract),
            reads=list(srck) + xk, writes=[("pbf", i % 2, q) for q in range(4)]))
        scr, okey = bufs[bi % 2]

        def fixops(src=src, srck=srck, scr=scr, okey=okey):
            P.add("dve", lambda e: e.tensor_tensor(out=scr[:, 16:16 + w - 1], in0=src[:, 16:16 + w - 1],
                                                   in1=icnt[:, 0:w - 1], op=ALU.mult),
                  reads=[srck[0], ("dv",)], writes=[okey[0]])
            P.add("dve", lambda e: e.tensor_tensor(out=pbf[:, 0:w - 1], in0=scr[:, 16:16 + w - 1],
                                                   in1=xp1[:, 16:16 + w - 1], op=ALU.subtract),
                  reads=[okey[0], xk[0]], writes=[("pbf", i % 2, 0)])
        ops.append(fixops)
        return ops

    def C_y(i):
        g = GORD[i]
        pbf = pbfs[i % 2]
        for q in range(4):
            bk = 4 + q
            P.add("pe", lambda e, q=q, bk=bk: e.matmul(bank(bk), lhsT=poolw[:, g, :], rhs=pbf[:, CH(q)],
                                                      start=True, stop=True),
                  reads=[("poolw",), ("pbf", i % 2, q), ("pbf", i % 2, 0)], writes=[PB(bk)])
            P.add("act", lambda e, q=q, bk=bk: e.activation(out=ypT[:, g, CH(q)], in_=bank(bk), func=AF.Copy,
                                                           scale=cvt[:, C_PSC + g:C_PSC + g + 1]),
                  reads=[PB(bk), ("cvt",)], writes=[("ypT", g, q)])

    mrgA = {}

    def D1_load(j):
        nonlocal rc
        if j < 8 and j not in mrgA:
            slot = rc % 2
            rc += 1
            load_ring(slot, WMRGA[j], 2048, None)
            mrgA[j] = (slot, ring[slot][:, 0:2048].rearrange("p (k n) -> p k n", n=128))

    def D1_ga(u):
        j, q = u // 4, u % 4
        slot, wblk = mrgA[j]
        bGA = u % 2

        def mm(e):
            ins = None
            for k in range(8):
                ins = e.matmul(bank(bGA), lhsT=wblk[:, k, :], rhs=hT[:, k, CH(q)], start=(k == 0), stop=(k == 7))
            return ins
        P.add("pe", mm, reads=[("ring", slot)] + [("hT", k, q) for k in range(8)], writes=[PB(bGA)])

    def D1_rest(u, with_ga=False):
        j, q = u // 4, u % 4
        slot, wblk = mrgA[j]
        pr = u % 2
        bGA, bBA = pr, 2 + pr

        def mm(e):
            ins = None
            if with_ga:
                for k in range(8):
                    ins = e.matmul(bank(bGA), lhsT=wblk[:, k, :], rhs=hT[:, k, CH(q)], start=(k == 0), stop=(k == 7))
            for k in range(8):
                ins = e.matmul(bank(bBA), lhsT=wblk[:, 8 + k, :], rhs=yT[:, k, CH(q)], start=(k == 0), stop=(k == 7))
            return ins
        rd = [("ring", slot)] + [("yT", k, q) for k in range(8)]
        wr = [PB(bBA)]
        if with_ga:
            rd += [("hT", k, q) for k in range(8)]
            wr.append(PB(bGA))
        P.add("pe", mm, reads=rd, writes=wr)
        P.add("act", lambda e: e.activation(out=tga[pr], in_=bank(bGA), func=AF.Tanh, scale=0.5,
                                            bias=dv[:, D_HBG + j:D_HBG + j + 1]),
              reads=[PB(bGA), ("dv",)], writes=[("tga", pr)])
        P.add("dve", lambda e: e.scalar_tensor_tensor(out=mergedT[:, j, CH(q)], in0=tga[pr], scalar=1.0, in1=bank(bBA),
                                                      op0=ALU.add, op1=ALU.mult),
              reads=[("tga", pr), PB(bBA)], writes=[("mergedT", j, q)])

    C_mm(0)
    D1_load(0)
    D1_ga(0)
    D1_ga(1)
    if stop == "B":
        return finish()
    P.fence(lambda e: e.memset(fsc[:, 0:1], 0.0),
            ["bufA", "bufI", "gel", "bufM", "xc", "hh", "xlb", "xlbpad", "xcb"],
            ["xp", "sA", "sB", "xppad", "tgb", "m2", "ypT", "mergedT"])
    P.fence(lambda e: e.memset(fsc[:, 1:2], 0.0), ["convd", "wabd", "wxbd"], ["tga", "pbf"])

    def padz(e):
        e.memset(xp1[:, 0:16], 0.0)
        e.memset(sA[:, 0:16], 0.0)
        return e.memset(sB[:, 0:16], 0.0)
    P.add("dve", padz, writes=[("xppad",)])
    P.fence(lambda e: e.memset(fsc[:, 3:4], 0.0), ["xinA", "hbA"], ["wo"])
    for hk in range(2):
        P.add("pool", lambda e, hk=hk: e.dma_start(
            out=woS[:, hk * 4:(hk + 1) * 4, :],
            in_=WO[:, hk * 4096:(hk + 1) * 4096].rearrange("p (a b) -> p a b", b=D)),
            writes=[("wo", hk)], dma="wo%d" % hk)

    C_evac(0)
    pend = C_pool_ops(0)
    sched = {7: ("y", 0, 1), 14: ("y", 1, 2), 20: ("y", 2, 3), 25: ("y", 3, None)}
    for u in range(32):
        j, q = u // 4, u % 4
        if q == 0:
            D1_load(j)
            D1_load(j + 1)
        if u >= 2:
            D1_ga(u)
        D1_rest(u)
        for _ in range(2):
            if pend:
                pend.pop(0)()
        if u in sched:
            _, iy, inext = sched[u]
            while pend:
                pend.pop(0)()
            C_y(iy)
            if inext is not None:
                C_mm(inext)
                C_evac(inext)
                pend = C_pool_ops(inext)
    assert not pend

    P.fence(lambda e: e.memset(fsc[:, 2:3], 0.0), ["poolw", "tga", "pbf", "pblk"], ["xinE"])
    mrgB = {}

    def D2_load(j):
        nonlocal rc
        if j < 8 and j not in mrgB:
            slot = rc % 2
            rc += 1
            load_ring(slot, WMRGB[j], 1536, None)
            mrgB[j] = (slot, ring[slot][:, 0:1536].rearrange("p (k n) -> p k n", n=128))

    for u in range(32):
        j, q = u // 4, u % 4
        if q == 0:
            D2_load(j)
            D2_load(j + 1)
        slot, wblk = mrgB[j]
        pr = u % 2
        bGB, bBB = 4 + pr, 6 + pr

        def mm(e, wblk=wblk, q=q, bGB=bGB, bBB=bBB):
            ins = None
            for k in range(8):
                ins = e.matmul(bank(bGB), lhsT=wblk[:, k, :], rhs=hT[:, k, CH(q)], start=(k == 0), stop=(k == 7))
            for k in range(4):
                ins = e.matmul(bank(bBB), lhsT=wblk[:, 8 + k, :], rhs=ypT[:, k, CH(q)], start=(k == 0), stop=(k == 3))
            return ins
        P.add("pe", mm, reads=[("ring", slot)] + [("hT", k, q) for k in range(8)] + [("ypT", k, q) for k in range(4)],
              writes=[PB(bGB), PB(bBB)])
        P.add("act", lambda e, j=j, pr=pr, bGB=bGB: e.activation(out=tgb[pr], in_=bank(bGB), func=AF.Tanh, scale=0.5,
                                                                bias=dv[:, D_HBG + 8 + j:D_HBG + 9 + j]),
              reads=[PB(bGB), ("dv",)], writes=[("tgb", pr)])
        P.add("dve", lambda e, pr=pr, bBB=bBB: e.scalar_tensor_tensor(out=m2b, in0=tgb[pr], scalar=1.0, in1=bank(bBB),
                                                                     op0=ALU.add, op1=ALU.mult),
              reads=[("tgb", pr), PB(bBB)], writes=[("m2",)])
        P.add("dve", lambda e, j=j, q=q: e.tensor_tensor(out=mergedT[:, j, CH(q)], in0=m2b, in1=mergedT[:, j, CH(q)],
                                                        op=ALU.add),
              reads=[("m2",), ("mergedT", j, q)], writes=[("mergedT", j, q)])

    act_warm(AF.Sqrt, 4)
    if stop == "D":
        return finish()
    P.fence(lambda e: e.memset(fsc[:, 4:5], 0.0), ["yT", "xp", "sA", "sB", "xppad", "tgb", "m2"], ["wff2", "f1rx"])
    wff2S = carve(0, 64 * K, BF16, "p (a b) -> p a b", b=D)
    def wkey(kg):
        return ("wff2c", kg) if kg == 7 else ("wff2", kg)

    def load_wff2(kg):
        P.add("pool", lambda e: e.dma_start(
            out=wff2S[:, kg * 4:(kg + 1) * 4, :],
            in_=WFF2[:, kg * 4096:(kg + 1) * 4096].rearrange("p (a b) -> p a b", b=D)),
            writes=[wkey(kg)], dma="wff2_%d" % kg)
    xinE = [carve(144 * K + s * 4 * K, 4 * K) for s in range(4)] + [carve(112 * K + s * 4 * K, 4 * K) for s in range(2)]
    tmpE = carve(120 * K, 4 * K)
    h2b = [carve(124 * K + s * 2 * K, 2 * K, BF16) for s in range(2)]
    NXE = 6

    def xkeyE(s):
        return ("xinE", s) if s < 4 else ("xinE2", s)

    def E_load(tt):
        s = tt % NXE
        P.add("sp", lambda e: e.dma_start(out=xinE[s], in_=X[tt * 128:(tt + 1) * 128, :]),
              writes=[xkeyE(s)], dma="xinE%d" % s)

    def E_mm(tt):
        pi = tt % 3

        def mm_o(e):
            ins = None
            for nh in range(2):
                for k in range(8):
                    ins = e.matmul(bank(2 * pi + nh), lhsT=mergedT[:, k, tt * 128:(tt + 1) * 128],
                                   rhs=woS[:, k, nh * 512:(nh + 1) * 512], start=(k == 0), stop=(k == 7))
            return ins
        P.add("pe", mm_o, reads=[("wo", 0), ("wo", 1)] + [("mergedT", k, tt // 4) for k in range(8)],
              writes=[PB(2 * pi), PB(2 * pi + 1)])

    def E_ss1(tt):
        pi = tt % 3
        P.add("act", lambda e: e.activation(out=junk[:], in_=pps[pi][:], func=AF.Square,
                                            accum_out=st[:, 48 + tt:48 + tt + 1]),
              reads=[PB(2 * pi), PB(2 * pi + 1)], writes=[("ss", 1, tt)])
        rstd_act(1, tt, D_EPS4, ("ss", 1, tt))

    def E_x1(tt):
        s = tt % NXE
        xin = xinE[s]
        pi = tt % 3
        rs = rstd_dve(1, tt)
        for nh in range(2):
            P.add("dve", lambda e, nh=nh: e.scalar_tensor_tensor(
                out=tmpE[:, nh * 512:(nh + 1) * 512], in0=bank(2 * pi + nh), scalar=rs,
                in1=gB[:, nh * 512:(nh + 1) * 512], op0=ALU.mult, op1=ALU.mult),
                reads=[PB(2 * pi + nh), ("rs", 1, tt), ("gB",)], writes=[("tmpE2", nh)])
        P.add("dve", lambda e: e.tensor_tensor(out=xin, in0=tmpE, in1=xin, op=ALU.add),
              reads=[("tmpE2", 0), ("tmpE2", 1), xkeyE(s)], writes=[xkeyE(s)])

    def E_ss2(tt):
        s = tt % NXE
        xin = xinE[s]
        P.add("act", lambda e: e.activation(out=junk[:], in_=xin, func=AF.Square, accum_out=st[:, 96 + tt:96 + tt + 1]),
              reads=[xkeyE(s)], writes=[("ss", 2, tt)])
        P.add("act", lambda e: e.dma_start(out=X1S[tt * 128:(tt + 1) * 128, :], in_=xin),
              reads=[xkeyE(s)], writes=[("x1s", tt)], dma="x1w%d" % s)
        rstd_act(2, tt, D_EPS1, ("ss", 2, tt))

    def E_h2(tt):
        s = tt % NXE
        xin = xinE[s]
        rs2 = rstd_dve(2, tt)
        hb = h2b[tt % 2]
        P.add("dve", lambda e: e.scalar_tensor_tensor(out=hb, in0=xin, scalar=rs2, in1=gA[:], op0=ALU.mult,
                                                      op1=ALU.mult),
              reads=[xkeyE(s), ("rs", 2, tt), ("gA",)], writes=[("h2b", tt % 2)])

    def E_tr(tt):
        transposes_to_hT(h2b[tt % 2], ("h2b", tt % 2), tt, 6 + tt % 2, act_only=True, part="pe")

    def E_trev(tt):
        transposes_to_hT(None, None, tt, 6 + tt % 2, act_only=True, part="evac")

    P.fence(lambda e: e.memset(fsc[:, 7:8], 0.0), ["ring"], ["xinE2", "tmpE2", "h2b"])
    for tt0 in range(2):
        E_load(tt0)

    hid = carve(64 * K, 64 * K, BF16, "p (a b) -> p a b", b=1024)
    NR = 4
    f1r = [carve(128 * K + s * 2 * K, 2 * K, BF16, "p (a b) -> p a b", b=128) for s in range(NR)]
    sqb = [carve(136 * K + s * 2 * K, 2 * K) for s in range(2)]
    x1in = [carve(140 * K + s * 4 * K, 4 * K) for s in range(2)]
    outst = [carve(148 * K + s * 4 * K, 4 * K) for s in range(2)]
    fstate = {"fu": 2, "blk": 0}

    def hkey(m, q2):
        return ("hidA" if m < 24 else "hidB", m, q2)

    f1rx = [carve(56 * K + s * 2 * K, 2 * K, BF16, "p (a b) -> p a b", b=128) for s in range(NR)]

    def f1sel(b):
        if b < NR:
            return f1rx[b], ("f1rx", b), "f1rx%d" % b
        slot = b % NR
        return f1r[slot], ("f1r", slot), "f1r%d" % slot

    def ff1_load(b):
        m = b % 32
        dst, key, dname = f1sel(b)
        P.add("pool", lambda e: e.dma_start(out=dst, in_=WFF1[m].rearrange("p (a b) -> p a b", b=128)),
              writes=[key], dma=dname)

    def ff1_block():
        b = fstate["blk"]
        fstate["blk"] += 1
        th, m = b // 32, b % 32
        wts, wk, _ = f1sel(b)
        if b >= fstate.get("preloaded", 0):
            ff1_load(b)
        if b == 22:
            P.fence(lambda e: e.memset(fsc[:, 7:8], 0.0), ["f1rx"], ["wff2c"], eng="pool")
        if b in (4, 10, 16, 22):
            load_wff2(4 + (b - 4) // 6)
        for q2 in range(2):
            bk = fstate["fu"] % 4
            sqs = fstate["fu"] % 2
            fstate["fu"] += 1
            q = th * 2 + q2

            def mm_f1(e, q=q, bk=bk):
                ins = None
                for k in range(8):
                    ins = e.matmul(bank(bk), lhsT=wts[:, k, :], rhs=hT[:, k, CH(q)], start=(k == 0), stop=(k == 7))
                return ins
            P.add("pe", mm_f1, reads=[wk] + [("hT", k, q) for k in range(8)], writes=[PB(bk)])
            P.add("act", lambda e, bk=bk, sqs=sqs: e.activation(out=sqb[sqs], in_=bank(bk), func=AF.Square),
                  reads=[PB(bk)], writes=[("sq", sqs)])
            P.add("dve", lambda e, q2=q2, bk=bk, sqs=sqs: e.scalar_tensor_tensor(
                out=hid[:, m, q2 * 512:(q2 + 1) * 512], in0=bank(bk), scalar=0.0, in1=sqb[sqs],
                op0=ALU.is_gt, op1=ALU.mult),
                reads=[PB(bk), ("sq", sqs)], writes=[hkey(m, q2)])

    def ok(t):
        return 0 <= t < NTT
    if stop is None:
        for b in range(NR):
            ff1_load(b)
        fstate["preloaded"] = NR
    for sidx in range(NTT + 6):
        if ok(sidx - 6):
            E_trev(sidx - 6)
        if sidx % 4 == 0 and sidx // 4 < 4:
            load_wff2(sidx // 4)
        if ok(sidx - 5):
            E_tr(sidx - 5)
        if ok(sidx):
            E_mm(sidx)
        if sidx == NTT - 1:
            P.fence(lambda e: e.memset(fsc[:, 5:6], 0.0), ["mergedT", "ypT", "wo"], ["hidA", "f1r", "sq"], eng="pool")
        if ok(sidx - 1):
            E_ss1(sidx - 1)
        if ok(sidx - 3):
            E_ss2(sidx - 3)
        if ok(sidx - 2):
            E_x1(sidx - 2)
        if ok(sidx - 4):
            E_h2(sidx - 4)
        if ok(sidx + 2):
            E_load(sidx + 2)
        if sidx >= NTT and stop is None:
            ff1_block()
            if sidx > NTT:
                ff1_block()

    if stop == "E":
        return finish()
    P.add("sp", lambda e: e.dma_start(out=gB[:], in_=GAINS[3:4, :].partition_broadcast(128)),
          writes=[("gB",)], dma="gB")
    P.fence(lambda e: e.memset(fsc[:, 6:7], 0.0),
            ["ring", "xinE", "tmpE", "h2b", "xinE2", "tmpE2"], ["hidB", "x1in", "outst"])
    for th in range(2):
        while fstate["blk"] < (th + 1) * 32:
            ff1_block()
        if stop == "F1" and th == 0:
            return finish()
        for tl in range(8):
            tt = th * 8 + tl
            s = tt % 2
            P.add("sp", lambda e, s=s, tt=tt: e.dma_start(out=x1in[s], in_=X1S[tt * 128:(tt + 1) * 128, :]),
                  reads=[("x1s", tt)], writes=[("x1in", s)], dma="x1in%d" % s)
            pi = 2 + tt % 2

            def mm_f2(e, tl=tl, pi=pi):
                ins = None
                for nh in range(2):
                    for k in range(32):
                        ins = e.matmul(bank(2 * pi + nh), lhsT=hid[:, k, tl * 128:(tl + 1) * 128],
                                       rhs=wff2S[:, k, nh * 512:(nh + 1) * 512], start=(k == 0), stop=(k == 31))
                return ins
            P.add("pe", mm_f2, reads=[wkey(kg) for kg in range(8)] + [hkey(k, tl // 4) for k in range(32)],
                  writes=[PB(2 * pi), PB(2 * pi + 1)])
            P.add("act", lambda e, tt=tt, pi=pi: e.activation(out=junk[:], in_=pps[pi][:], func=AF.Square,
                                                             accum_out=st[:, 144 + tt:144 + tt + 1]),
                  reads=[PB(2 * pi), PB(2 * pi + 1)], writes=[("ss", 3, tt)])
            rs = rstd_ops(3, tt, D_EPS1, ("ss", 3, tt))
            for nh in range(2):
                P.add("dve", lambda e, pi=pi, rs=rs, s=s, nh=nh: e.scalar_tensor_tensor(
                    out=outst[s][:, nh * 512:(nh + 1) * 512], in0=bank(2 * pi + nh), scalar=rs,
                    in1=gB[:, nh * 512:(nh + 1) * 512], op0=ALU.mult, op1=ALU.mult),
                    reads=[PB(2 * pi + nh), ("rs", 3, tt), ("gB",)], writes=[("outst", s, nh)])
            for nh in range(2):
                hs = slice(nh * 512, (nh + 1) * 512)
                P.add("dve", lambda e, s=s, hs=hs: e.tensor_tensor(out=outst[s][:, hs], in0=outst[s][:, hs],
                                                                   in1=x1in[s][:, hs], op=ALU.add),
                      reads=[("outst", s, nh), ("x1in", s)], writes=[("outst", s, nh)])
                P.add("sp", lambda e, s=s, tt=tt, hs=hs: e.dma_start(out=OUT[tt * 128:(tt + 1) * 128, hs],
                                                                    in_=outst[s][:, hs]),
                      reads=[("outst", s, nh)], dma="out%d_%d" % (s, nh))
        if stop == "F2" and th == 0:
            return finish()

    return finish()


def _colvec(v):
    v = np.asarray(v, np.float32).reshape(-1, 128)
    return np.ascontiguousarray(v.T)


def _prep_shared(inp):
    f = lambda a: np.ascontiguousarray(np.asarray(a, np.float32))
    w_in = f(inp["w_in"])[0]
    w_lru_up = f(inp["w_lru_up"])[0]
    w_pool_up = f(inp["w_pool_up"])[0]
    w_o = f(inp["w_o"])[0]
    w_ff1 = f(inp["w_ff1"])[0]
    w_ff2 = f(inp["w_ff2"])[0]

    def ktile(w):
        kk = w.shape[0] // 128
        return w.reshape(kk, 128, w.shape[1]).transpose(1, 0, 2)

    wlru = np.empty((8, 128, 8, 256), np.float32)
    for c in range(8):
        wlru[c, :, :, 0:128] = ktile(w_in[:, c * 128:(c + 1) * 128])
        wlru[c, :, :, 128:256] = ktile(w_in[:, 1024 + c * 128:1024 + (c + 1) * 128])
    wpool = np.empty((4, 128, 8, 128), np.float32)
    for g in range(4):
        wpool[g] = ktile(w_in[:, 2048 + g * 128:2048 + (g + 1) * 128])
    wmrga = np.empty((8, 128, 16, 128), np.float32)
    wmrgb = np.empty((8, 128, 12, 128), np.float32)
    for j in range(8):
        wmrga[j, :, 0:8] = ktile(w_in[:, 2560 + j * 128:2560 + (j + 1) * 128])
        wmrga[j, :, 8:16] = ktile(w_lru_up[:, j * 128:(j + 1) * 128])
        wmrgb[j, :, 0:8] = ktile(w_in[:, 3584 + j * 128:3584 + (j + 1) * 128])
        wmrgb[j, :, 8:12] = ktile(w_pool_up[:, j * 128:(j + 1) * 128])
    wff1 = np.empty((32, 128, 8, 128), np.float32)
    for m in range(32):
        wff1[m] = ktile(w_ff1[:, m * 128:(m + 1) * 128])

    def blockdiag(w):
        out = np.zeros((128, 8, 128), np.float32)
        for h in range(16):
            c, o = h // 2, (h % 2) * 64
            out[o:o + 64, c, o:o + 64] = w[h]
        return out
    wabd = blockdiag(f(inp["lru_w_a"])[0])
    wxbd = blockdiag(f(inp["lru_w_x"])[0])
    conv_w = f(inp["conv_w"])[0]
    convd = np.zeros((128, 32, 128), np.float32)
    idx = np.arange(128)
    for c in range(8):
        for k4 in range(4):
            convd[idx, c * 4 + k4, idx] = conv_w[k4, c * 128:(c + 1) * 128]
    poolw = f(inp["pool_w"])[0].transpose(1, 0, 2)
    colvecs = np.concatenate([
        _colvec(inp["conv_b"]), _colvec(inp["lru_b_a"]), _colvec(inp["lru_b_x"]), _colvec(inp["lru_lambda"]),
        _colvec(inp["pool_scale"]), _colvec(inp["b_gate"])], axis=1)
    gains = np.stack([f(inp["norm_mix_pre"])[0], f(inp["norm_mix_post"])[0],
                      f(inp["norm_mlp_pre"])[0], f(inp["norm_mlp_post"])[0]], 0)
    return {
        "wlru": np.ascontiguousarray(wlru.reshape(8, 128, 2048)),
        "wpool": np.ascontiguousarray(wpool.reshape(4, 128, 1024)),
        "wmrga": np.ascontiguousarray(wmrga.reshape(8, 128, 2048)),
        "wmrgb": np.ascontiguousarray(wmrgb.reshape(8, 128, 1536)),
        "wo": np.ascontiguousarray(ktile(w_o).reshape(128, 8192)),
        "wff1": np.ascontiguousarray(wff1.reshape(32, 128, 1024)),
        "wff2": np.ascontiguousarray(ktile(w_ff2).reshape(128, 32768)),
        "wabd": np.ascontiguousarray(wabd.reshape(128, 1024)),
        "wxbd": np.ascontiguousarray(wxbd.reshape(128, 1024)),
        "convd": np.ascontiguousarray(convd.reshape(128, 4096)),
        "poolw": np.ascontiguousarray(poolw.reshape(128, 512)),
        "ident": np.eye(128, dtype=np.float32),
        "colvecs": np.ascontiguousarray(colvecs.astype(np.float32)),
        "gains": np.ascontiguousarray(gains),
    }


_NC_CACHE = {}


def kernel(**inputs):
    x = np.ascontiguousarray(np.asarray(inputs["x"], np.float32))
    shared = _prep_shared(inputs)
    if "nc" not in _NC_CACHE:
        _NC_CACHE["nc"] = build_nc()
    nc = _NC_CACHE["nc"]
    in_maps = []
    for b in range(NCORES):
        m = dict(shared)
        m["x"] = x[b]
        in_maps.append(m)
    res = run_bass_kernel_spmd(nc, in_maps, core_ids=list(range(NCORES)))
    out = np.stack([np.asarray(r["out"], np.float32) for r in res.results], 0)
    return out
```

```python
import numpy as np
from contextlib import ExitStack
import concourse.bass as bass
import concourse.mybir as mybir
from concourse.bass_utils import run_bass_kernel_spmd

F32 = mybir.dt.float32
BF16 = mybir.dt.bfloat16
AF = mybir.ActivationFunctionType
ALU = mybir.AluOpType

T = 2048
D = 1024
NCORES = 8
EPS = 1e-6
K = 1024


class Prog:
    ENGS = ("sp", "act", "pool", "dve", "pe")
    MAX_SWDGE = 4

    def __init__(self, nc, es):
        self.nc = nc
        self.es = es
        self.ops = []
        self.lw = {}
        self.rd = {}
        self.dma_names = []
        self.floors = {}
        self.pool_dmas = []

    def add(self, eng, fn, reads=(), writes=(), dma=None):
        deps = set()
        for k in list(reads) + list(writes):
            f = self.floors.get(k[0])
            if f is not None:
                deps.add(f)
        for k in reads:
            if k in self.lw:
                deps.add(self.lw[k])
        for k in writes:
            if k in self.lw:
                deps.add(self.lw[k])
            deps |= self.rd.get(k, set())
        idx = len(self.ops)
        if dma is not None and eng == "pool":
            if len(self.pool_dmas) >= self.MAX_SWDGE:
                deps.add(self.pool_dmas[-self.MAX_SWDGE])
            self.pool_dmas.append(idx)
        self.ops.append(dict(eng=eng, fn=fn, deps=deps, dma=dma, sig=False))
        if dma is not None and dma not in self.dma_names:
            self.dma_names.append(dma)
        for k in reads:
            self.rd.setdefault(k, set()).add(idx)
        for k in writes:
            self.lw[k] = idx
            self.rd[k] = set()
        return idx

    def fence(self, fn, old_names, new_names, eng="dve"):
        deps = set()
        old = set(old_names)
        for k, i in self.lw.items():
            if k[0] in old:
                deps.add(i)
        for k, s in self.rd.items():
            if k[0] in old:
                deps |= s
        for n in old:
            f = self.floors.get(n)
            if f is not None:
                deps.add(f)
        idx = len(self.ops)
        self.ops.append(dict(eng=eng, fn=fn, deps=deps, dma=None, sig=False))
        for n in new_names:
            self.floors[n] = idx
        return idx

    def emit(self, final_wait=()):
        nc, es = self.nc, self.es
        ops = self.ops

        def skip(po, c):
            return po["dma"] is None and c["dma"] is None and po["eng"] == "pe" and c["eng"] == "pe"

        for c in ops:
            for p in c["deps"]:
                if not skip(ops[p], c):
                    ops[p]["sig"] = True
        esem = {e: es.enter_context(nc.semaphore("s_" + e)) for e in self.ENGS}
        dsem = {n: es.enter_context(nc.semaphore("d_" + str(n))) for n in self.dma_names}
        cnt = {e: 0 for e in self.ENGS}
        dcnt = {n: 0 for n in self.dma_names}
        for o in ops:
            if o["dma"] is not None:
                dcnt[o["dma"]] += 16
                o["sem"], o["val"] = dsem[o["dma"]], dcnt[o["dma"]]
            elif o["sig"]:
                cnt[o["eng"]] += 1
                o["sem"], o["val"] = esem[o["eng"]], cnt[o["eng"]]
        per = {e: [o for o in ops if o["eng"] == e] for e in self.ENGS}

        def run(e, eng):
            waited = {}
            for o in per[e]:
                need = {}
                for p in o["deps"]:
                    po = ops[p]
                    if "sem" not in po or skip(po, o):
                        continue
                    s = po["sem"]
                    if po["val"] > need.get(id(s), (None, 0))[1]:
                        need[id(s)] = (s, po["val"])
                for key, (s, v) in need.items():
                    if waited.get(key, 0) >= v:
                        continue
                    eng.wait_ge(s, v)
                    waited[key] = v
                ins = o["fn"](eng)
                if "sem" in o:
                    ins.then_inc(o["sem"], 16 if o["dma"] is not None else 1)
            if e == "sp":
                for n in final_wait:
                    eng.wait_ge(dsem[n], dcnt[n])

        with nc.Block() as block:
            @block.sync
            def _(eng):
                run("sp", eng)

            @block.scalar
            def _(eng):
                run("act", eng)

            @block.gpsimd
            def _(eng):
                run("pool", eng)

            @block.vector
            def _(eng):
                run("dve", eng)

            @block.tensor
            def _(eng):
                run("pe", eng)


def build_nc(stop=None):
    nc = bass.Bass("TRN2", target_bir_lowering=False)
    es = ExitStack()

    def dram(name, shape, kind="ExternalInput"):
        return nc.dram_tensor(name, shape, F32, kind=kind).ap()

    X = dram("x", [T, D])
    OUT = dram("out", [T, D], "ExternalOutput")
    WLRU = dram("wlru", [8, 128, 2048])
    WPOOL = dram("wpool", [4, 128, 1024])
    WMRGA = dram("wmrga", [8, 128, 2048])
    WMRGB = dram("wmrgb", [8, 128, 1536])
    WO = dram("wo", [128, 8192])
    WFF1 = dram("wff1", [32, 128, 1024])
    WFF2 = dram("wff2", [128, 32768])
    WABD = dram("wabd", [128, 1024])
    WXBD = dram("wxbd", [128, 1024])
    CONVD = dram("convd", [128, 4096])
    POOLW = dram("poolw", [128, 512])
    IDENT = dram("ident", [128, 128])
    CVEC = dram("colvecs", [128, 52])
    GAINS = dram("gains", [4, D])
    X1S = dram("x1s", [T, D], "Internal")

    def sb(name, shape, dt):
        return es.enter_context(nc.sbuf_tensor(name, shape, dt))

    cvt = sb("cvt", [128, 52], F32)
    dv = sb("dv", [128, 64], F32)
    st = sb("st", [128, 4 * 48], F32)
    icnt = sb("icnt", [128, 16], F32)
    gA = sb("gA", [128, D], F32)
    gB = sb("gB", [128, D], F32)
    identb = sb("identb", [128, 128], BF16)
    junk = sb("junk", [128, D], BF16)
    fsc = sb("fsc", [128, 8], F32)
    hT = sb("hT", [128, 8, T], BF16)
    BIGN = 160 * 256
    big = sb("big", [128, BIGN], F32)
    pps = [es.enter_context(nc.psum_tensor("pp%d" % i, [128, 1024], F32)) for i in range(4)]

    def bank(i):
        return pps[i // 2][:, (i % 2) * 512:(i % 2) * 512 + 512]

    def bank_bf(i):
        return bank(i).bitcast(BF16)

    def carve(off, nbytes, dt=F32, pat=None, **kw):
        assert off % 4 == 0 and nbytes % 4 == 0 and off + nbytes <= BIGN * 4, (off, nbytes)
        ap = big[:, off // 4:(off + nbytes) // 4]
        if dt == BF16:
            ap = ap.bitcast(BF16)
        if pat is not None:
            ap = ap.rearrange(pat, **kw)
        return ap

    C_CONVB, C_BA, C_BX, C_LAM, C_PSC, C_BG = 0, 8, 16, 24, 32, 36
    D_HBA, D_HBX, D_HBG, D_CH, D_T0, D_EPS1, D_EPS4 = 0, 8, 16, 32, 40, 48, 49

    P = Prog(nc, es)
    warm = sb("warm", [128, 8], F32)

    def act_warm(func, col):
        P.add("act", lambda e: e.activation(out=warm[:, col + 1:col + 2], in_=warm[:, 0:1], func=func),
              reads=[("warm",)], writes=[("warmo", col)])

    def finish():
        P.emit(final_wait=[n for n in ("out0_0", "out0_1", "out1_0", "out1_1") if n in P.dma_names])
        return nc

    def PB(i):
        return ("pb", i)

    yT = carve(0, 32 * K, BF16, "p (a b) -> p a b", b=T)
    mergedT = carve(64 * K, 32 * K, BF16, "p (a b) -> p a b", b=T)
    ypT = carve(96 * K, 16 * K, BF16, "p (a b) -> p a b", b=T)
    ring = [carve(112 * K + s * 8 * K, 8 * K, BF16) for s in range(2)]
    woS = carve(128 * K, 16 * K, BF16, "p (a b) -> p a b", b=D)
    wabd = carve(144 * K, 2 * K, BF16, "p (a b) -> p a b", b=128)
    wxbd = carve(146 * K, 2 * K, BF16, "p (a b) -> p a b", b=128)
    convd = carve(148 * K, 8 * K, BF16, "p (a b) -> p a b", b=128)
    poolw = carve(156 * K, 1 * K, BF16, "p (a b) -> p a b", b=128)
    xinA = [carve(128 * K + s * 4 * K, 4 * K) for s in range(3)]
    hbA = [carve(140 * K + s * 2 * K, 2 * K, BF16) for s in range(2)]

    P.add("dve", lambda e: e.memset(warm[:, 0:1], 1.0), writes=[("warm",)])
    act_warm(AF.Sqrt, 0)
    xin_first = [carve(128 * K + s * 4 * K, 4 * K) for s in range(2)]
    for tt0 in range(2):
        P.add("sp", lambda e, tt0=tt0: e.dma_start(out=xin_first[tt0], in_=X[tt0 * 128:(tt0 + 1) * 128, :]),
              writes=[("xinA", tt0)], dma="xinA%d" % tt0)
    P.add("sp", lambda e: e.dma_start(out=cvt[:], in_=CVEC), writes=[("cvt",)], dma="cv")
    P.add("sp", lambda e: e.dma_start(out=gA[:], in_=GAINS[0:1, :].partition_broadcast(128)),
          writes=[("gA",)], dma="gA")
    P.add("pool", lambda e: e.dma_start(out=identb[:], in_=IDENT), writes=[("ident",)], dma="ident")

    def late_const_loads(after_key):
        P.add("pool", lambda e: e.dma_start(out=convd, in_=CONVD.rearrange("p (a b) -> p a b", b=128)),
              reads=[after_key], writes=[("convd",)], dma="convd")
        P.add("pool", lambda e: e.dma_start(out=wabd, in_=WABD.rearrange("p (a b) -> p a b", b=128)),
              writes=[("wabd",)], dma="wabd")
        P.add("pool", lambda e: e.dma_start(out=wxbd, in_=WXBD.rearrange("p (a b) -> p a b", b=128)),
              writes=[("wxbd",)], dma="wxbd")
        P.add("pool", lambda e: e.dma_start(out=poolw, in_=POOLW.rearrange("p (a b) -> p a b", b=128)),
              writes=[("poolw",)], dma="poolw")

    def setup_dv(e):
        e.memset(dv[:, D_EPS1:D_EPS1 + 1], EPS)
        e.memset(dv[:, D_EPS4:D_EPS4 + 1], 4.0 * EPS)
        for t in range(16):
            e.memset(icnt[:, t:t + 1], 1.0 / (t + 1))
        e.tensor_scalar(out=dv[:, D_HBA:D_HBA + 16], in0=cvt[:, C_BA:C_BA + 16], scalar1=0.5, scalar2=None,
                        op0=ALU.mult)
        return e.tensor_scalar(out=dv[:, D_HBG:D_HBG + 16], in0=cvt[:, C_BG:C_BG + 16], scalar1=0.5,
                               scalar2=None, op0=ALU.mult)
    P.add("dve", setup_dv, reads=[("cvt",)], writes=[("dv",)])
    def softplus_setup():
        P.add("act", lambda e: e.activation(out=dv[:, D_T0:D_T0 + 8], in_=cvt[:, C_LAM:C_LAM + 8], func=AF.Exp,
                                            scale=-1.0), reads=[("cvt",)], writes=[("dvt",)])
        P.add("act", lambda e: e.activation(out=dv[:, D_T0:D_T0 + 8], in_=dv[:, D_T0:D_T0 + 8], func=AF.Ln,
                                            bias=1.0), reads=[("dvt",)], writes=[("dvt",)])
        P.add("dve", lambda e: e.tensor_scalar(out=dv[:, D_CH:D_CH + 8], in0=dv[:, D_T0:D_T0 + 8], scalar1=-4.0,
                                               scalar2=None, op0=ALU.mult), reads=[("dvt",)], writes=[("dvc",)])

    def rstd_act(n, col, eps_col, ss_key):
        base = n * 48
        P.add("act", lambda e: e.activation(out=st[:, base + 16 + col:base + 17 + col],
                                            in_=st[:, base + col:base + col + 1], func=AF.Sqrt,
                                            scale=1.0 / D, bias=dv[:, eps_col:eps_col + 1]),
              reads=[ss_key, ("dv",)], writes=[("sd", n, col)])

    def rstd_dve(n, col):
        base = n * 48
        P.add("dve", lambda e: e.reciprocal(out=st[:, base + 32 + col:base + 33 + col],
                                            in_=st[:, base + 16 + col:base + 17 + col]),
              reads=[("sd", n, col)], writes=[("rs", n, col)])
        return st[:, base + 32 + col:base + 33 + col]

    def rstd_ops(n, col, eps_col, ss_key):
        rstd_act(n, col, eps_col, ss_key)
        return rstd_dve(n, col)

    def transposes_to_hT(src_ap, src_key, tt, bk, act_only=False, part="both"):
        if part in ("both", "pe"):
            transposes_pe(src_ap, src_key, bk)
        if part in ("both", "evac"):
            transposes_evac(tt, bk, act_only)

    def transposes_pe(src_ap, src_key, bk):
        def tr(e):
            ins = None
            for c in range(8):
                ins = e.transpose(out=bank_bf(bk)[:, c * 128:(c + 1) * 128], in_=src_ap[:, c * 128:(c + 1) * 128],
                                  identity=identb[:])
            return ins
        P.add("pe", tr, reads=[src_key, ("ident",)], writes=[PB(bk)])

    def transposes_evac(tt, bk, act_only):
        outap = hT[:, :, tt * 128:(tt + 1) * 128]
        inap = bank_bf(bk).rearrange("p (a b) -> p a b", b=128)
        wkeys = [("hT", c, tt // 4) for c in range(8)]
        if tt % 2 == 0 or act_only:
            P.add("act", lambda e: e.activation(out=outap, in_=inap, func=AF.Copy), reads=[PB(bk)], writes=wkeys)
        else:
            P.add("dve", lambda e: e.tensor_copy(out=outap, in_=inap), reads=[PB(bk)], writes=wkeys)

    NTT = T // 128
    WKB = 32 * K
    xinA6 = xinA + [carve(s * 4 * K, 4 * K) for s in range(3)]

    def xkeyA(s):
        return ("xinA", s) if s < 3 else ("xinA2", s)

    def A_stage1(tt):
        s = tt % 6
        xin = xinA6[s]
        if tt >= 2:
            P.add("sp", lambda e: e.dma_start(out=xin, in_=X[tt * 128:(tt + 1) * 128, :]),
                  writes=[xkeyA(s)], dma="xinA%d" % s)
        if tt == 4:
            late_const_loads(xkeyA(3))
        if tt == 6:
            softplus_setup()
        P.add("act", lambda e: e.activation(out=junk[:], in_=xin, func=AF.Square, accum_out=st[:, tt:tt + 1]),
              reads=[xkeyA(s)], writes=[("ss", 0, tt)])
        rstd_act(0, tt, D_EPS1, ("ss", 0, tt))

    def A_stage2(tt):
        s = tt % 6
        xin = xinA6[s]
        rs = rstd_dve(0, tt)
        hb = hbA[tt % 2]
        P.add("dve", lambda e: e.scalar_tensor_tensor(out=hb, in0=xin, scalar=rs, in1=gA[:], op0=ALU.mult,
                                                      op1=ALU.mult),
              reads=[xkeyA(s), ("rs", 0, tt), ("gA",)], writes=[("hbA", tt % 2)])
        transposes_to_hT(hb, ("hbA", tt % 2), tt, tt % 2, part="pe")

    def A_stage3(tt):
        transposes_to_hT(None, None, tt, tt % 2, part="evac")

    for sidx in range(NTT + 2):
        if sidx < NTT:
            A_stage1(sidx)
        if 1 <= sidx <= NTT:
            A_stage2(sidx - 1)
        if sidx >= 2:
            A_stage3(sidx - 2)

    if stop == "A":
        return finish()
    act_warm(AF.Gelu_apprx_tanh, 2)
    P.add("sp", lambda e: e.dma_start(out=gB[:], in_=GAINS[1:2, :].partition_broadcast(128)),
          writes=[("gB",)], dma="gB")
    P.add("sp", lambda e: e.dma_start(out=gA[:], in_=GAINS[2:3, :].partition_broadcast(128)),
          writes=[("gA",)], dma="gA")
    P.fence(lambda e: e.memset(fsc[:, 6:7], 0.0), ["xinA2"], ["yT"])

    bufA = [carve(WKB + p * 8 * K, 8 * K) for p in range(2)]
    bufI = [carve(WKB + 16 * K + p * 8 * K, 8 * K) for p in range(2)]
    gel = [carve(WKB + 32 * K + p * 8 * K, 8 * K) for p in range(3)]
    bufM = carve(WKB + 56 * K, 8 * K)
    xcr = [carve(WKB + 64 * K + p * 2 * K, 2 * K) for p in range(2)]
    hhT = bufM
    xlb = carve(WKB + 68 * K, 4608, BF16)
    xcb = [carve(WKB + 73 * K + p * K, K, BF16) for p in range(2)]

    P.add("dve", lambda e: e.memset(xlb[:, 0:4], 0.0), writes=[("xlbpad",)])

    def CH(q):
        return slice(q * 512, (q + 1) * 512)

    def load_ring(slot, src_ap, n_el, key):
        dst = ring[slot][:, 0:n_el]
        P.add("pool", lambda e: e.dma_start(out=dst, in_=src_ap), writes=[("ring", slot)], dma="ring%d" % slot)

    rc = 0
    lru_blk = {}

    def B_stage1(u):
        nonlocal rc
        c, q = u // 4, u % 4
        par = c % 2
        if q == 0:
            for cc in (c, c + 1):
                if cc < 8 and cc not in lru_blk:
                    slot = rc % 2
                    rc += 1
                    load_ring(slot, WLRU[cc], 2048, None)
                    lru_blk[cc] = (slot, ring[slot][:, 0:2048].rearrange("p (k n) -> p k n", n=256))
        slot, wblk = lru_blk[c]
        bXA, bGB = u % 2, 2 + u % 2

        def mm_x(e):
            ins = None
            for k in range(8):
                ins = e.matmul(bank(bXA), lhsT=wblk[:, k, 0:128], rhs=hT[:, k, CH(q)], start=(k == 0), stop=(k == 7))
            for k in range(8):
                ins = e.matmul(bank(bGB), lhsT=wblk[:, k, 128:256], rhs=hT[:, k, CH(q)], start=(k == 0), stop=(k == 7))
            return ins
        P.add("pe", mm_x, reads=[("ring", slot)] + [("hT", k, q) for k in range(8)], writes=[PB(bXA), PB(bGB)])
        P.add("dve", lambda e: e.tensor_copy(out=xlb[:, 3 + q * 512:3 + (q + 1) * 512], in_=bank(bXA)),
              reads=[PB(bXA), ("xlbpad",)], writes=[("xlb", q)])
        P.add("act", lambda e: e.activation(out=gel[c % 3][:, CH(q)], in_=bank(bGB), func=AF.Gelu_apprx_tanh),
              reads=[PB(bGB)], writes=[("gel", c % 3, q)])

    def B_stage2(u):
        c, q = u // 4, u % 4
        bCV = 4 + u % 2

        def mm_conv(e):
            ins = None
            for k4 in range(4):
                ins = e.matmul(bank(bCV), lhsT=convd[:, c * 4 + k4, :], rhs=xlb[:, q * 512 + k4:q * 512 + k4 + 512],
                               start=(k4 == 0), stop=(k4 == 3))
            return ins
        P.add("pe", mm_conv, reads=[("convd",), ("xlb", q)] + ([("xlb", q - 1)] if q > 0 else [("xlbpad",)]),
              writes=[PB(bCV)])
        xc = xcr[u % 2]
        P.add("dve", lambda e: e.tensor_scalar(out=xc, in0=bank(bCV), scalar1=cvt[:, C_CONVB + c:C_CONVB + c + 1],
                                               scalar2=None, op0=ALU.add),
              reads=[PB(bCV), ("cvt",)], writes=[("xc", u % 2)])
        P.add("dve", lambda e: e.tensor_copy(out=xcb[u % 2], in_=xc), reads=[("xc", u % 2)], writes=[("xcb", u % 2)])

    def B_stage3(u):
        c = u // 4

        def mm_g(e):
            e.matmul(bank(6), lhsT=wabd[:, c, :], rhs=xcb[u % 2], start=True, stop=True)
            return e.matmul(bank(7), lhsT=wxbd[:, c, :], rhs=xcb[u % 2], start=True, stop=True)
        P.add("pe", mm_g, reads=[("wabd",), ("wxbd",), ("xcb", u % 2)], writes=[PB(6), PB(7)])

    def B_stage4(u):
        c, q = u // 4, u % 4
        par = c % 2
        xc = xcr[u % 2]
        P.add("act", lambda e: e.activation(out=bufA[par][:, CH(q)], in_=bank(6), func=AF.Tanh, scale=0.5,
                                            bias=dv[:, D_HBA + c:D_HBA + c + 1]),
              reads=[PB(6), ("dv",)], writes=[("bufA", par, q)])
        P.add("act", lambda e: e.activation(out=bufI[par][:, CH(q)], in_=bank(7), func=AF.Tanh, scale=0.5,
                                            bias=dv[:, D_HBX + c:D_HBX + c + 1]),
              reads=[PB(7), ("dv",)], writes=[("bufI", par, q)])
        P.add("dve", lambda e: e.scalar_tensor_tensor(out=bufI[par][:, CH(q)], in0=bufI[par][:, CH(q)], scalar=1.0,
                                                      in1=xc, op0=ALU.add, op1=ALU.mult),
              reads=[("bufI", par, q), ("xc", u % 2)], writes=[("bufI", par, q)])

    def B_back1(c):
        par = c % 2
        for q in range(4):
            P.add("act", lambda e, q=q: e.activation(out=bufA[par][:, CH(q)], in_=bufA[par][:, CH(q)], func=AF.Exp,
                                                     scale=dv[:, D_CH + c:D_CH + c + 1], bias=dv[:, D_CH + c:D_CH + c + 1]),
                  reads=[("bufA", par, q), ("dvc",)], writes=[("bufA", par, q)])
        for q in range(4):
            P.add("act", lambda e, q=q: e.activation(out=bufM[:, CH(q)], in_=bufA[par][:, CH(q)], func=AF.Square),
                  reads=[("bufA", par, q)], writes=[("bufM", q)])

    def B_back2(c):
        par = c % 2
        for q in range(4):
            P.add("act", lambda e, q=q: e.activation(out=bufM[:, CH(q)], in_=bufM[:, CH(q)], func=AF.Sqrt, scale=-1.0,
                                                     bias=1.0),
                  reads=[("bufM", q)], writes=[("bufM", q)])
        for q in range(4):
            P.add("dve", lambda e, q=q: e.tensor_tensor(out=bufI[par][:, CH(q)], in0=bufI[par][:, CH(q)],
                                                        in1=bufM[:, CH(q)], op=ALU.mult),
                  reads=[("bufI", par, q), ("bufM", q)], writes=[("bufI", par, q)])

    def B_back3(c):
        par = c % 2
        P.add("dve", lambda e: e.tensor_tensor_scan(out=hhT, data0=bufA[par], data1=bufI[par], initial=0.0,
                                                    op0=ALU.mult, op1=ALU.add),
              reads=[("bufA", par, q) for q in range(4)] + [("bufI", par, q) for q in range(4)]
              + [("bufM", q) for q in range(4)], writes=[("bufM", q) for q in range(4)])
        P.add("dve", lambda e: e.scalar_tensor_tensor(out=yT[:, c, :], in0=hhT, scalar=0.5, in1=gel[c % 3],
                                                      op0=ALU.mult, op1=ALU.mult),
              reads=[("bufM", q) for q in range(4)] + [("gel", c % 3, q) for q in range(4)],
              writes=[("yT", c, q) for q in range(4)])

    NU = 32
    for sidx in range(NU + 8):
        u4 = sidx - 3
        if 0 <= u4 < NU:
            B_stage4(u4)
            if u4 % 4 == 3:
                B_back1(u4 // 4)
        u5 = sidx - 4
        if 0 <= u5 < NU and u5 % 4 == 3:
            B_back2(u5 // 4)
        u6 = sidx - 6
        if 0 <= u6 < NU and u6 % 4 == 3:
            B_back3(u6 // 4)
        if sidx < NU:
            B_stage1(sidx)
        if 0 <= sidx - 1 < NU:
            B_stage2(sidx - 1)
        if 0 <= sidx - 2 < NU:
            B_stage3(sidx - 2)

    WINS = (2, 4, 8, 16)
    GORD = (3, 2, 1, 0)
    pblk = carve(157 * K, 2 * K, BF16, "p (k n) -> p k n", n=128)
    tga = [carve(144 * K + p * 2 * K, 2 * K) for p in range(2)]
    pbfs = [carve(148 * K + i * 4 * K, 4 * K, BF16) for i in range(2)]
    XPW = 2064
    XPB = 8448
    xp1 = carve(WKB, XPW * 4)
    sA = carve(WKB + XPB, XPW * 4)
    sB = carve(WKB + 2 * XPB, XPW * 4)
    tgb = [carve(WKB + 3 * XPB + p * 2 * K, 2 * K) for p in range(2)]
    m2b = carve(WKB + 3 * XPB + 4 * K, 2 * K)

    def C_mm(i):
        g = GORD[i]
        P.add("pool", lambda e: e.dma_start(out=pblk, in_=WPOOL[g].rearrange("p (k n) -> p k n", n=128)),
              writes=[("pblk",)], dma="pblk")
        for q in range(4):
            bk = 4 + q

            def mm_p(e, q=q, bk=bk):
                ins = None
                for k in range(8):
                    ins = e.matmul(bank(bk), lhsT=pblk[:, k, :], rhs=hT[:, k, CH(q)], start=(k == 0), stop=(k == 7))
                return ins
            P.add("pe", mm_p, reads=[("pblk",)] + [("hT", k, q) for k in range(8)], writes=[PB(bk)])

    def C_evac(i):
        for q in range(4):
            bk = 4 + q
            P.add("act", lambda e, q=q, bk=bk: e.activation(out=xp1[:, 16 + q * 512:16 + (q + 1) * 512], in_=bank(bk),
                                                           func=AF.Copy),
                  reads=[PB(bk), ("xppad",)], writes=[("xp", q)])

    def C_pool_ops(i):
        g = GORD[i]
        w = WINS[g]
        pbf = pbfs[i % 2]
        xk = [("xp", q) for q in range(4)]
        ops = []
        src, srck = xp1, xk
        sh = 1
        bufs = [(sA, [("sA", q) for q in range(4)]), (sB, [("sB", q) for q in range(4)])]
        bi = 0
        while sh < w:
            dst, dstk = bufs[bi % 2]
            ops.append(lambda src=src, dst=dst, sh=sh, srck=srck, dstk=dstk: P.add(
                "dve", lambda e: e.tensor_tensor(out=dst[:, 16:16 + T], in0=src[:, 16:16 + T],
                                                 in1=src[:, 16 - sh:16 - sh + T], op=ALU.add),
                reads=list(srck) + [("xppad",)], writes=list(dstk)))
            src, srck = dst, dstk
            sh *= 2
            bi += 1
        ops.append(lambda src=src, srck=srck: P.add(
            "dve", lambda e: e.scalar_tensor_tensor(out=pbf, in0=src[:, 16:16 + T], scalar=1.0 / w,
                                                    in1=xp1[:, 16:16 + T], op0=ALU.mult, op1=ALU.subtract),
            reads=list(srck) + xk, writes=[("pbf", i % 2, q) for q in range(4)]))
        scr, okey = bufs[bi % 2]

        def fixops(src=src, srck=srck, scr=scr, okey=okey):
            P.add("dve", lambda e: e.tensor_tensor(out=scr[:, 16:16 + w - 1], in0=src[:, 16:16 + w - 1],
                                                   in1=icnt[:, 0:w - 1], op=ALU.mult),
                  reads=[srck[0], ("dv",)], writes=[okey[0]])
            P.add("dve", lambda e: e.tensor_tensor(out=pbf[:, 0:w - 1], in0=scr[:, 16:16 + w - 1],
                                                   in1=xp1[:, 16:16 + w - 1], op=ALU.subtract),
                  reads=[okey[0], xk[0]], writes=[("pbf", i % 2, 0)])
        ops.append(fixops)
        return ops

    def C_y(i):
        g = GORD[i]
        pbf = pbfs[i % 2]
        for q in range(4):
            bk = 4 + q
            P.add("pe", lambda e, q=q, bk=bk: e.matmul(bank(bk), lhsT=poolw[:, g, :], rhs=pbf[:, CH(q)],
                                                      start=True, stop=True),
                  reads=[("poolw",), ("pbf", i % 2, q), ("pbf", i % 2, 0)], writes=[PB(bk)])
            P.add("act", lambda e, q=q, bk=bk: e.activation(out=ypT[:, g, CH(q)], in_=bank(bk), func=AF.Copy,
                                                           scale=cvt[:, C_PSC + g:C_PSC + g + 1]),
                  reads=[PB(bk), ("cvt",)], writes=[("ypT", g, q)])

    mrgA = {}

    def D1_load(j):
        nonlocal rc
        if j < 8 and j not in mrgA:
            slot = rc % 2
            rc += 1
            load_ring(slot, WMRGA[j], 2048, None)
            mrgA[j] = (slot, ring[slot][:, 0:2048].rearrange("p (k n) -> p k n", n=128))

    def D1_ga(u):
        j, q = u // 4, u % 4
        slot, wblk = mrgA[j]
        bGA = u % 2

        def mm(e):
            ins = None
            for k in range(8):
                ins = e.matmul(bank(bGA), lhsT=wblk[:, k, :], rhs=hT[:, k, CH(q)], start=(k == 0), stop=(k == 7))
            return ins
        P.add("pe", mm, reads=[("ring", slot)] + [("hT", k, q) for k in range(8)], writes=[PB(bGA)])

    def D1_rest(u, with_ga=False):
        j, q = u // 4, u % 4
        slot, wblk = mrgA[j]
        pr = u % 2
        bGA, bBA = pr, 2 + pr

        def mm(e):
            ins = None
            if with_ga:
                for k in range(8):
                    ins = e.matmul(bank(bGA), lhsT=wblk[:, k, :], rhs=hT[:, k, CH(q)], start=(k == 0), stop=(k == 7))
            for k in range(8):
                ins = e.matmul(bank(bBA), lhsT=wblk[:, 8 + k, :], rhs=yT[:, k, CH(q)], start=(k == 0), stop=(k == 7))
            return ins
        rd = [("ring", slot)] + [("yT", k, q) for k in range(8)]
        wr = [PB(bBA)]
        if with_ga:
            rd += [("hT", k, q) for k in range(8)]
            wr.append(PB(bGA))
        P.add("pe", mm, reads=rd, writes=wr)
        P.add("act", lambda e: e.activation(out=tga[pr], in_=bank(bGA), func=AF.Tanh, scale=0.5,
                                            bias=dv[:, D_HBG + j:D_HBG + j + 1]),
              reads=[PB(bGA), ("dv",)], writes=[("tga", pr)])
        P.add("dve", lambda e: e.scalar_tensor_tensor(out=mergedT[:, j, CH(q)], in0=tga[pr], scalar=1.0, in1=bank(bBA),
                                                      op0=ALU.add, op1=ALU.mult),
              reads=[("tga", pr), PB(bBA)], writes=[("mergedT", j, q)])

    mrgB = {}

    def D2_load(j):
        nonlocal rc
        if j < 8 and j not in mrgB:
            slot = rc % 2
            rc += 1
            load_ring(slot, WMRGB[j], 1536, None)
            mrgB[j] = (slot, ring[slot][:, 0:1536].rearrange("p (k n) -> p k n", n=128))

    C_mm(0)
    D1_load(0)
    D1_ga(0)
    D1_ga(1)
    if stop == "B":
        return finish()
    P.fence(lambda e: e.memset(fsc[:, 0:1], 0.0),
            ["bufA", "bufI", "gel", "bufM", "xc", "hh", "xlb", "xlbpad", "xcb"],
            ["xp", "sA", "sB", "xppad", "tgb", "m2", "ypT", "mergedT"])
    P.fence(lambda e: e.memset(fsc[:, 1:2], 0.0), ["convd", "wabd", "wxbd"], ["tga", "pbf"])

    def padz(e):
        e.memset(xp1[:, 0:16], 0.0)
        e.memset(sA[:, 0:16], 0.0)
        return e.memset(sB[:, 0:16], 0.0)
    P.add("dve", padz, writes=[("xppad",)])
    D1_load(1)
    P.fence(lambda e: e.memset(fsc[:, 3:4], 0.0), ["xinA", "hbA"], ["wo"])
    for hk in range(2):
        P.add("pool", lambda e, hk=hk: e.dma_start(
            out=woS[:, hk * 4:(hk + 1) * 4, :],
            in_=WO[:, hk * 4096:(hk + 1) * 4096].rearrange("p (a b) -> p a b", b=D)),
            writes=[("wo", hk)], dma="wo%d" % hk)

    C_evac(0)
    pend = C_pool_ops(0)
    sched = {7: ("y", 0, 1), 14: ("y", 1, 2), 20: ("y", 2, 3), 25: ("y", 3, None)}
    for u in range(32):
        j, q = u // 4, u % 4
        if q == 0:
            D1_load(j)
            D1_load(j + 1)
            if u == 28:
                D2_load(0)
        if u >= 2:
            D1_ga(u)
        D1_rest(u)
        for _ in range(2):
            if pend:
                pend.pop(0)()
        if u in sched:
            _, iy, inext = sched[u]
            while pend:
                pend.pop(0)()
            C_y(iy)
            if inext is not None:
                C_mm(inext)
                C_evac(inext)
                pend = C_pool_ops(inext)
    assert not pend

    P.fence(lambda e: e.memset(fsc[:, 2:3], 0.0), ["poolw", "tga", "pbf", "pblk"], ["xinE"])

    for u in range(32):
        j, q = u // 4, u % 4
        if q == 0:
            D2_load(j)
            D2_load(j + 1)
        slot, wblk = mrgB[j]
        pr = u % 2
        bGB, bBB = 4 + pr, 6 + pr

        def mm(e, wblk=wblk, q=q, bGB=bGB, bBB=bBB):
            ins = None
            for k in range(8):
                ins = e.matmul(bank(bGB), lhsT=wblk[:, k, :], rhs=hT[:, k, CH(q)], start=(k == 0), stop=(k == 7))
            for k in range(4):
                ins = e.matmul(bank(bBB), lhsT=wblk[:, 8 + k, :], rhs=ypT[:, k, CH(q)], start=(k == 0), stop=(k == 3))
            return ins
        P.add("pe", mm, reads=[("ring", slot)] + [("hT", k, q) for k in range(8)] + [("ypT", k, q) for k in range(4)],
              writes=[PB(bGB), PB(bBB)])
        P.add("act", lambda e, j=j, pr=pr, bGB=bGB: e.activation(out=tgb[pr], in_=bank(bGB), func=AF.Tanh, scale=0.5,
                                                                bias=dv[:, D_HBG + 8 + j:D_HBG + 9 + j]),
              reads=[PB(bGB), ("dv",)], writes=[("tgb", pr)])
        P.add("dve", lambda e, pr=pr, bBB=bBB: e.scalar_tensor_tensor(out=m2b, in0=tgb[pr], scalar=1.0, in1=bank(bBB),
                                                                     op0=ALU.add, op1=ALU.mult),
              reads=[("tgb", pr), PB(bBB)], writes=[("m2",)])
        P.add("dve", lambda e, j=j, q=q: e.tensor_tensor(out=mergedT[:, j, CH(q)], in0=m2b, in1=mergedT[:, j, CH(q)],
                                                        op=ALU.add),
              reads=[("m2",), ("mergedT", j, q)], writes=[("mergedT", j, q)])

    act_warm(AF.Sqrt, 4)
    if stop == "D":
        return finish()
    P.fence(lambda e: e.memset(fsc[:, 4:5], 0.0), ["yT", "xp", "sA", "sB", "xppad", "tgb", "m2"], ["wff2", "f1rx"])
    wff2S = carve(0, 64 * K, BF16, "p (a b) -> p a b", b=D)
    def wkey(kg):
        return ("wff2c", kg) if kg == 7 else ("wff2", kg)

    def load_wff2(kg):
        P.add("pool", lambda e: e.dma_start(
            out=wff2S[:, kg * 4:(kg + 1) * 4, :],
            in_=WFF2[:, kg * 4096:(kg + 1) * 4096].rearrange("p (a b) -> p a b", b=D)),
            writes=[wkey(kg)], dma="wff2_%d" % kg)
    xinE = [carve(144 * K + s * 4 * K, 4 * K) for s in range(4)] + [carve(112 * K + s * 4 * K, 4 * K) for s in range(2)]
    tmpE = carve(120 * K, 4 * K)
    h2b = [carve(124 * K + s * 2 * K, 2 * K, BF16) for s in range(2)]
    NXE = 6

    def xkeyE(s):
        return ("xinE", s) if s < 4 else ("xinE2", s)

    def E_load(tt):
        s = tt % NXE
        P.add("sp", lambda e: e.dma_start(out=xinE[s], in_=X[tt * 128:(tt + 1) * 128, :]),
              writes=[xkeyE(s)], dma="xinE%d" % s)

    def E_mm(tt):
        pi = tt % 3

        def mm_o(e):
            ins = None
            for nh in range(2):
                for k in range(8):
                    ins = e.matmul(bank(2 * pi + nh), lhsT=mergedT[:, k, tt * 128:(tt + 1) * 128],
                                   rhs=woS[:, k, nh * 512:(nh + 1) * 512], start=(k == 0), stop=(k == 7))
            return ins
        P.add("pe", mm_o, reads=[("wo", 0), ("wo", 1)] + [("mergedT", k, tt // 4) for k in range(8)],
              writes=[PB(2 * pi), PB(2 * pi + 1)])

    def E_ss1(tt):
        pi = tt % 3
        P.add("act", lambda e: e.activation(out=junk[:], in_=pps[pi][:], func=AF.Square,
                                            accum_out=st[:, 48 + tt:48 + tt + 1]),
              reads=[PB(2 * pi), PB(2 * pi + 1)], writes=[("ss", 1, tt)])
        rstd_act(1, tt, D_EPS4, ("ss", 1, tt))

    def E_x1(tt):
        s = tt % NXE
        xin = xinE[s]
        pi = tt % 3
        rs = rstd_dve(1, tt)
        for nh in range(2):
            P.add("dve", lambda e, nh=nh: e.scalar_tensor_tensor(
                out=tmpE[:, nh * 512:(nh + 1) * 512], in0=bank(2 * pi + nh), scalar=rs,
                in1=gB[:, nh * 512:(nh + 1) * 512], op0=ALU.mult, op1=ALU.mult),
                reads=[PB(2 * pi + nh), ("rs", 1, tt), ("gB",)], writes=[("tmpE2", nh)])
        P.add("dve", lambda e: e.tensor_tensor(out=xin, in0=tmpE, in1=xin, op=ALU.add),
              reads=[("tmpE2", 0), ("tmpE2", 1), xkeyE(s)], writes=[xkeyE(s)])

    def E_ss2(tt):
        s = tt % NXE
        xin = xinE[s]
        P.add("act", lambda e: e.activation(out=junk[:], in_=xin, func=AF.Square, accum_out=st[:, 96 + tt:96 + tt + 1]),
              reads=[xkeyE(s)], writes=[("ss", 2, tt)])
        P.add("act", lambda e: e.dma_start(out=X1S[tt * 128:(tt + 1) * 128, :], in_=xin),
              reads=[xkeyE(s)], writes=[("x1s", tt)], dma="x1w%d" % s)
        rstd_act(2, tt, D_EPS1, ("ss", 2, tt))

    def E_h2(tt):
        s = tt % NXE
        xin = xinE[s]
        rs2 = rstd_dve(2, tt)
        hb = h2b[tt % 2]
        P.add("dve", lambda e: e.scalar_tensor_tensor(out=hb, in0=xin, scalar=rs2, in1=gA[:], op0=ALU.mult,
                                                      op1=ALU.mult),
              reads=[xkeyE(s), ("rs", 2, tt), ("gA",)], writes=[("h2b", tt % 2)])

    def E_tr(tt):
        transposes_to_hT(h2b[tt % 2], ("h2b", tt % 2), tt, 6 + tt % 2, act_only=True, part="pe")

    def E_trev(tt):
        transposes_to_hT(None, None, tt, 6 + tt % 2, act_only=True, part="evac")

    P.fence(lambda e: e.memset(fsc[:, 7:8], 0.0), ["ring"], ["xinE2", "tmpE2", "h2b"])
    for tt0 in range(2):
        E_load(tt0)

    hid = carve(64 * K, 64 * K, BF16, "p (a b) -> p a b", b=1024)
    NR = 4
    f1r = [carve(128 * K + s * 2 * K, 2 * K, BF16, "p (a b) -> p a b", b=128) for s in range(NR)]
    sqb = [carve(136 * K + s * 2 * K, 2 * K) for s in range(2)]
    x1in = [carve(140 * K + s * 4 * K, 4 * K) for s in range(2)]
    outst = [carve(148 * K + s * 4 * K, 4 * K) for s in range(2)]
    fstate = {"fu": 2, "blk": 0}

    def hkey(m, q2):
        return ("hidA" if m < 24 else "hidB", m, q2)

    f1rx = [carve(56 * K + s * 2 * K, 2 * K, BF16, "p (a b) -> p a b", b=128) for s in range(NR)]

    def f1sel(b):
        if b < NR:
            return f1rx[b], ("f1rx", b), "f1rx%d" % b
        slot = b % NR
        return f1r[slot], ("f1r", slot), "f1r%d" % slot

    def ff1_load(b):
        m = b % 32
        dst, key, dname = f1sel(b)
        P.add("pool", lambda e: e.dma_start(out=dst, in_=WFF1[m].rearrange("p (a b) -> p a b", b=128)),
              writes=[key], dma=dname)

    def ff1_block():
        b = fstate["blk"]
        fstate["blk"] += 1
        th, m = b // 32, b % 32
        wts, wk, _ = f1sel(b)
        if b >= fstate.get("preloaded", 0):
            ff1_load(b)
        if b == 22:
            P.fence(lambda e: e.memset(fsc[:, 7:8], 0.0), ["f1rx"], ["wff2c"], eng="pool")
        if b in (4, 10, 16, 22):
            load_wff2(4 + (b - 4) // 6)
        for q2 in range(2):
            bk = fstate["fu"] % 4
            sqs = fstate["fu"] % 2
            fstate["fu"] += 1
            q = th * 2 + q2

            def mm_f1(e, q=q, bk=bk):
                ins = None
                for k in range(8):
                    ins = e.matmul(bank(bk), lhsT=wts[:, k, :], rhs=hT[:, k, CH(q)], start=(k == 0), stop=(k == 7))
                return ins
            P.add("pe", mm_f1, reads=[wk] + [("hT", k, q) for k in range(8)], writes=[PB(bk)])
            P.add("act", lambda e, bk=bk, sqs=sqs: e.activation(out=sqb[sqs], in_=bank(bk), func=AF.Square),
                  reads=[PB(bk)], writes=[("sq", sqs)])
            P.add("dve", lambda e, q2=q2, bk=bk, sqs=sqs: e.scalar_tensor_tensor(
                out=hid[:, m, q2 * 512:(q2 + 1) * 512], in0=bank(bk), scalar=0.0, in1=sqb[sqs],
                op0=ALU.is_gt, op1=ALU.mult),
                reads=[PB(bk), ("sq", sqs)], writes=[hkey(m, q2)])

    def ok(t):
        return 0 <= t < NTT
    if stop is None:
        for b in range(NR):
            ff1_load(b)
        fstate["preloaded"] = NR
    for sidx in range(NTT + 6):
        if ok(sidx - 6):
            E_trev(sidx - 6)
        if sidx % 4 == 0 and sidx // 4 < 4:
            load_wff2(sidx // 4)
        if ok(sidx - 5):
            E_tr(sidx - 5)
        if ok(sidx):
            E_mm(sidx)
        if sidx == NTT - 1:
            P.fence(lambda e: e.memset(fsc[:, 5:6], 0.0), ["mergedT", "ypT", "wo"], ["hidA", "f1r", "sq"], eng="pool")
        if ok(sidx - 1):
            E_ss1(sidx - 1)
        if ok(sidx - 3):
            E_ss2(sidx - 3)
        if ok(sidx - 2):
            E_x1(sidx - 2)
        if ok(sidx - 4):
            E_h2(sidx - 4)
        if ok(sidx + 2):
            E_load(sidx + 2)
        if sidx >= NTT and stop is None:
            ff1_block()
            if sidx > NTT:
                ff1_block()

    if stop == "E":
        return finish()
    P.add("sp", lambda e: e.dma_start(out=gB[:], in_=GAINS[3:4, :].partition_broadcast(128)),
          writes=[("gB",)], dma="gB")
    P.fence(lambda e: e.memset(fsc[:, 6:7], 0.0),
            ["ring", "xinE", "tmpE", "h2b", "xinE2", "tmpE2"], ["hidB", "x1in", "outst"])
    for th in range(2):
        while fstate["blk"] < (th + 1) * 32:
            ff1_block()
        if stop == "F1" and th == 0:
            return finish()
        for tl in range(8):
            tt = th * 8 + tl
            s = tt % 2
            P.add("sp", lambda e, s=s, tt=tt: e.dma_start(out=x1in[s], in_=X1S[tt * 128:(tt + 1) * 128, :]),
                  reads=[("x1s", tt)], writes=[("x1in", s)], dma="x1in%d" % s)
            pi = 2 + tt % 2

            def mm_f2(e, tl=tl, pi=pi):
                ins = None
                for nh in range(2):
                    for k in range(32):
                        ins = e.matmul(bank(2 * pi + nh), lhsT=hid[:, k, tl * 128:(tl + 1) * 128],
                                       rhs=wff2S[:, k, nh * 512:(nh + 1) * 512], start=(k == 0), stop=(k == 31))
                return ins
            P.add("pe", mm_f2, reads=[wkey(kg) for kg in range(8)] + [hkey(k, tl // 4) for k in range(32)],
                  writes=[PB(2 * pi), PB(2 * pi + 1)])
            P.add("act", lambda e, tt=tt, pi=pi: e.activation(out=junk[:], in_=pps[pi][:], func=AF.Square,
                                                             accum_out=st[:, 144 + tt:144 + tt + 1]),
                  reads=[PB(2 * pi), PB(2 * pi + 1)], writes=[("ss", 3, tt)])
            rs = rstd_ops(3, tt, D_EPS1, ("ss", 3, tt))
            for nh in range(2):
                P.add("dve", lambda e, pi=pi, rs=rs, s=s, nh=nh: e.scalar_tensor_tensor(
                    out=outst[s][:, nh * 512:(nh + 1) * 512], in0=bank(2 * pi + nh), scalar=rs,
                    in1=gB[:, nh * 512:(nh + 1) * 512], op0=ALU.mult, op1=ALU.mult),
                    reads=[PB(2 * pi + nh), ("rs", 3, tt), ("gB",)], writes=[("outst", s, nh)])
            for nh in range(2):
                hs = slice(nh * 512, (nh + 1) * 512)
                P.add("dve", lambda e, s=s, hs=hs: e.tensor_tensor(out=outst[s][:, hs], in0=outst[s][:, hs],
                                                                   in1=x1in[s][:, hs], op=ALU.add),
                      reads=[("outst", s, nh), ("x1in", s)], writes=[("outst", s, nh)])
                P.add("sp", lambda e, s=s, tt=tt, hs=hs: e.dma_start(out=OUT[tt * 128:(tt + 1) * 128, hs],
                                                                    in_=outst[s][:, hs]),
                      reads=[("outst", s, nh)], dma="out%d_%d" % (s, nh))
        if stop == "F2" and th == 0:
            return finish()

    return finish()


def _colvec(v):
    v = np.asarray(v, np.float32).reshape(-1, 128)
    return np.ascontiguousarray(v.T)


def _prep_shared(inp):
    f = lambda a: np.ascontiguousarray(np.asarray(a, np.float32))
    w_in = f(inp["w_in"])[0]
    w_lru_up = f(inp["w_lru_up"])[0]
    w_pool_up = f(inp["w_pool_up"])[0]
    w_o = f(inp["w_o"])[0]
    w_ff1 = f(inp["w_ff1"])[0]
    w_ff2 = f(inp["w_ff2"])[0]

    def ktile(w):
        kk = w.shape[0] // 128
        return w.reshape(kk, 128, w.shape[1]).transpose(1, 0, 2)

    wlru = np.empty((8, 128, 8, 256), np.float32)
    for c in range(8):
        wlru[c, :, :, 0:128] = ktile(w_in[:, c * 128:(c + 1) * 128])
        wlru[c, :, :, 128:256] = ktile(w_in[:, 1024 + c * 128:1024 + (c + 1) * 128])
    wpool = np.empty((4, 128, 8, 128), np.float32)
    for g in range(4):
        wpool[g] = ktile(w_in[:, 2048 + g * 128:2048 + (g + 1) * 128])
    wmrga = np.empty((8, 128, 16, 128), np.float32)
    wmrgb = np.empty((8, 128, 12, 128), np.float32)
    for j in range(8):
        wmrga[j, :, 0:8] = ktile(w_in[:, 2560 + j * 128:2560 + (j + 1) * 128])
        wmrga[j, :, 8:16] = ktile(w_lru_up[:, j * 128:(j + 1) * 128])
        wmrgb[j, :, 0:8] = ktile(w_in[:, 3584 + j * 128:3584 + (j + 1) * 128])
        wmrgb[j, :, 8:12] = ktile(w_pool_up[:, j * 128:(j + 1) * 128])
    wff1 = np.empty((32, 128, 8, 128), np.float32)
    for m in range(32):
        wff1[m] = ktile(w_ff1[:, m * 128:(m + 1) * 128])

    def blockdiag(w):
        out = np.zeros((128, 8, 128), np.float32)
        for h in range(16):
            c, o = h // 2, (h % 2) * 64
            out[o:o + 64, c, o:o + 64] = w[h]
        return out
    wabd = blockdiag(f(inp["lru_w_a"])[0])
    wxbd = blockdiag(f(inp["lru_w_x"])[0])
    conv_w = f(inp["conv_w"])[0]
    convd = np.zeros((128, 32, 128), np.float32)
    idx = np.arange(128)
    for c in range(8):
        for k4 in range(4):
            convd[idx, c * 4 + k4, idx] = conv_w[k4, c * 128:(c + 1) * 128]
    poolw = f(inp["pool_w"])[0].transpose(1, 0, 2)
    colvecs = np.concatenate([
        _colvec(inp["conv_b"]), _colvec(inp["lru_b_a"]), _colvec(inp["lru_b_x"]), _colvec(inp["lru_lambda"]),
        _colvec(inp["pool_scale"]), _colvec(inp["b_gate"])], axis=1)
    gains = np.stack([f(inp["norm_mix_pre"])[0], f(inp["norm_mix_post"])[0],
                      f(inp["norm_mlp_pre"])[0], f(inp["norm_mlp_post"])[0]], 0)
    return {
        "wlru": np.ascontiguousarray(wlru.reshape(8, 128, 2048)),
        "wpool": np.ascontiguousarray(wpool.reshape(4, 128, 1024)),
        "wmrga": np.ascontiguousarray(wmrga.reshape(8, 128, 2048)),
        "wmrgb": np.ascontiguousarray(wmrgb.reshape(8, 128, 1536)),
        "wo": np.ascontiguousarray(ktile(w_o).reshape(128, 8192)),
        "wff1": np.ascontiguousarray(wff1.reshape(32, 128, 1024)),
        "wff2": np.ascontiguousarray(ktile(w_ff2).reshape(128, 32768)),
        "wabd": np.ascontiguousarray(wabd.reshape(128, 1024)),
        "wxbd": np.ascontiguousarray(wxbd.reshape(128, 1024)),
        "convd": np.ascontiguousarray(convd.reshape(128, 4096)),
        "poolw": np.ascontiguousarray(poolw.reshape(128, 512)),
        "ident": np.eye(128, dtype=np.float32),
        "colvecs": np.ascontiguousarray(colvecs.astype(np.float32)),
        "gains": np.ascontiguousarray(gains),
    }


_NC_CACHE = {}


def kernel(**inputs):
    x = np.ascontiguousarray(np.asarray(inputs["x"], np.float32))
    shared = _prep_shared(inputs)
    if "nc" not in _NC_CACHE:
        _NC_CACHE["nc"] = build_nc()
    nc = _NC_CACHE["nc"]
    in_maps = []
    for b in range(NCORES):
        m = dict(shared)
        m["x"] = x[b]
        in_maps.append(m)
    res = run_bass_kernel_spmd(nc, in_maps, core_ids=list(range(NCORES)))
    out = np.stack([np.asarray(r["out"], np.float32) for r in res.results], 0)
    return out
```
